# Optimizing a Trainium2 kernel written in Bass

```python
import math
import jax, jax.numpy as jnp
from jax import lax
import numpy as np

D_MODEL = 1024
BATCH = 8
SEQ = 2048
DEPTH = 2

GRID_W = 64
CTX_LEN = 256

D_MIX = D_MODEL
D_CONV = D_MODEL // 4
D_CONF = D_MODEL // 4
NA_HEAD_DIM = 64
D_NA = D_MIX - D_CONV - D_CONF
N_NA_HEADS = D_NA // NA_HEAD_DIM
SHORT_CONV_W = 3
CONF_CONV_W = 31
NA_WIN_ROWS_MAX = 8
NA_WIN_COLS = 16
N_EXPERTS = 16
EC_CAPACITY_FACTOR = 2
D_EXPERT = 1024
LN_EPS = 1e-5
DEEPNORM_ALPHA = (2.0 * DEPTH) ** 0.25
DEEPNORM_BETA = (8.0 * DEPTH) ** -0.25
NEG_INF = -1e30

OFF_A = 0
OFF_B = OFF_A + 3 * D_CONV
OFF_Q = OFF_B + 2 * D_CONF
OFF_K = OFF_Q + D_NA
OFF_V = OFF_K + D_NA
D_IN = OFF_V + D_NA

kernel_name = "hybrid_conv_conformer_natten_ecmoe_deepnorm"


def layer_norm(x, gain=None, bias=None):
    xf = x.astype(jnp.float32)
    mu = jnp.mean(xf, axis=-1, keepdims=True)
    var = jnp.mean(jnp.square(xf - mu), axis=-1, keepdims=True)
    y = (xf - mu) * lax.rsqrt(var + LN_EPS)
    if gain is not None:
        y = y * gain.astype(jnp.float32) + bias.astype(jnp.float32)
    return y.astype(x.dtype)


def modulation(cond, w_mod, b_mod):
    m = jax.nn.silu(cond) @ w_mod + b_mod
    return jnp.split(m, 6, axis=-1)


def modulate(h, shift, scale):
    return h * (1 + scale) + shift


def depthwise_conv(x, w):
    pad = w.shape[0] // 2
    return lax.conv_general_dilated(
        x, w[:, None, :].astype(x.dtype), window_strides=(1,), padding=[(pad, pad)],
        dimension_numbers=('NWC', 'WIO', 'NWC'), feature_group_count=x.shape[-1])


def short_conv_mixer(u, w_short):
    bg, cg, xv = jnp.split(u, 3, axis=-1)
    return bg * depthwise_conv(cg * xv, w_short)


def conformer_conv_mixer(u, w_dw, b_dw, g_ln, b_ln):
    a, g = jnp.split(u, 2, axis=-1)
    h = a * jax.nn.sigmoid(g)
    h = depthwise_conv(h, w_dw) + b_dw
    return jax.nn.silu(layer_norm(h, g_ln, b_ln))


def split_heads(t):
    return t.reshape(*t.shape[:-1], N_NA_HEADS, NA_HEAD_DIM)


def na_tables(rows):
    wr = min(NA_WIN_ROWS_MAX, rows)
    r = np.arange(rows)
    row_start = np.clip(r - wr // 2, 0, rows - wr)
    row_idx = row_start[:, None] + np.arange(wr)[None, :]
    d_row = row_idx - r[:, None] + (NA_WIN_ROWS_MAX - 1)
    j = np.arange(GRID_W)
    col_start = np.clip(j - NA_WIN_COLS // 2, 0, GRID_W - NA_WIN_COLS)
    col_in = (j[None, :] >= col_start[:, None]) & (j[None, :] < col_start[:, None] + NA_WIN_COLS)
    d_col = np.clip(j[None, :] - j[:, None] + NA_WIN_COLS - 1, 0, 2 * NA_WIN_COLS - 2)
    return wr, row_idx, d_row, col_in, d_col


def neighbourhood_attention(q, k, v, k_ctx, v_ctx, rpb, tables):
    wr, row_idx, d_row, col_in, d_col = tables
    bsz, seq = q.shape[0], q.shape[1]
    rows = seq // GRID_W
    grid = lambda t: t.reshape(bsz, rows, GRID_W, N_NA_HEADS, NA_HEAD_DIM)
    qg, kg, vg = grid(q), grid(k), grid(v)
    k_band = kg[:, row_idx]
    v_band = vg[:, row_idx]
    s_band = jnp.einsum('brqhd,brwkhd->bhrqwk', qg, k_band).astype(jnp.float32)
    bias = rpb.astype(jnp.float32)[:, d_row[:, None, :, None], d_col[None, :, None, :]]
    bias = jnp.where(col_in[None, None, :, None, :], bias, NEG_INF)
    s_band = (s_band + bias[None]).reshape(bsz, N_NA_HEADS, rows, GRID_W, wr * GRID_W)
    s_ctx = jnp.einsum('brqhd,bchd->bhrqc', qg, k_ctx).astype(jnp.float32)
    p = jax.nn.softmax(jnp.concatenate([s_band, s_ctx], axis=-1), axis=-1).astype(v.dtype)
    p_band = p[..., :wr * GRID_W].reshape(bsz, N_NA_HEADS, rows, GRID_W, wr, GRID_W)
    p_ctx = p[..., wr * GRID_W:]
    o = (jnp.einsum('bhrqwk,brwkhd->brqhd', p_band, v_band)
         + jnp.einsum('bhrqc,bchd->brqhd', p_ctx, v_ctx))
    return o.reshape(bsz, seq, D_NA)


def context_attention(q, k, v):
    s = jnp.einsum('bqhd,bkhd->bhqk', q, k).astype(jnp.float32)
    p = jax.nn.softmax(s, axis=-1).astype(v.dtype)
    o = jnp.einsum('bhqk,bkhd->bqhd', p, v)
    return o.reshape(q.shape[0], q.shape[1], D_NA)


def expert_choice_ffn(h, w_router, w_gate, w_up, w_down):
    bsz, n_tok, d = h.shape
    cap = EC_CAPACITY_FACTOR * n_tok // N_EXPERTS
    aff = jax.nn.softmax((h @ w_router).astype(jnp.float32), axis=-1)
    g, idx = lax.top_k(jnp.swapaxes(aff, 1, 2), cap)
    b_idx = jnp.arange(bsz)[:, None, None]
    xs = h[b_idx, idx]
    a = jnp.einsum('becd,edf->becf', xs, w_gate)
    u = jnp.einsum('becd,edf->becf', xs, w_up)
    y = jnp.einsum('becf,efd->becd', jax.nn.silu(a) * u, w_down)
    y = y * g[..., None].astype(y.dtype)
    flat = (b_idx * n_tok + idx).reshape(-1)
    out = jax.ops.segment_sum(y.reshape(-1, d), flat, num_segments=bsz * n_tok)
    return out.reshape(bsz, n_tok, d)


def post_norm(x, y, gate, g, b):
    return layer_norm(DEEPNORM_ALPHA * x + (1 + gate) * y, g, b)


def setup_inputs(seed: int = 0) -> dict:
    key = jax.random.key(seed)
    ks = jax.random.split(key, 32)
    f32 = jnp.float32
    L = DEPTH

    def nrm(k, shape, s):
        return jax.random.normal(k, shape, f32) * s

    return {
        "x": nrm(ks[0], (BATCH, SEQ, D_MODEL), 1.0),
        "c": nrm(ks[1], (BATCH, D_MODEL), 1.0),
        "ctx": nrm(ks[2], (BATCH, CTX_LEN, D_MODEL), 1.0),
        "c_ctx": nrm(ks[3], (D_MODEL,), 1.0),
        "w_mod": nrm(ks[4], (L, D_MODEL, 6 * D_MODEL), 0.1 * D_MODEL ** -0.5),
        "b_mod": nrm(ks[5], (L, 6 * D_MODEL), 0.01),
        "w_in": nrm(ks[6], (L, D_MODEL, D_IN), D_MODEL ** -0.5),
        "b_in": nrm(ks[7], (L, D_IN), 0.01),
        "w_short": nrm(ks[8], (L, SHORT_CONV_W, D_CONV), SHORT_CONV_W ** -0.5),
        "w_conf_dw": nrm(ks[9], (L, CONF_CONV_W, D_CONF), CONF_CONV_W ** -0.5),
        "b_conf_dw": nrm(ks[10], (L, D_CONF), 0.01),
        "g_conf_ln": 1.0 + nrm(ks[11], (L, D_CONF), 0.01),
        "b_conf_ln": nrm(ks[12], (L, D_CONF), 0.01),
        "na_rpb": nrm(ks[13], (L, N_NA_HEADS, 2 * NA_WIN_ROWS_MAX - 1, 2 * NA_WIN_COLS - 1), 0.1),
        "w_out": nrm(ks[14], (L, D_MIX, D_MODEL), DEEPNORM_BETA * D_MIX ** -0.5),
        "b_out": nrm(ks[15], (L, D_MODEL), 0.01),
        "g_post1": 1.0 + nrm(ks[16], (L, D_MODEL), 0.01),
        "b_post1": nrm(ks[17], (L, D_MODEL), 0.01),
        "w_router": nrm(ks[18], (L, D_MODEL, N_EXPERTS), D_MODEL ** -0.5),
        "w_gate": nrm(ks[19], (L, N_EXPERTS, D_MODEL, D_EXPERT), D_MODEL ** -0.5),
        "w_up": nrm(ks[20], (L, N_EXPERTS, D_MODEL, D_EXPERT), D_MODEL ** -0.5),
        "w_down": nrm(ks[21], (L, N_EXPERTS, D_EXPERT, D_MODEL), DEEPNORM_BETA * D_EXPERT ** -0.5),
        "g_post2": 1.0 + nrm(ks[22], (L, D_MODEL), 0.01),
        "b_post2": nrm(ks[23], (L, D_MODEL), 0.01),
    }


def reference(x, c, ctx, c_ctx, w_mod, b_mod, w_in, b_in, w_short, w_conf_dw, b_conf_dw, g_conf_ln, b_conf_ln,
              na_rpb, w_out, b_out, g_post1, b_post1, w_router, w_gate, w_up, w_down, g_post2, b_post2):
    rows = x.shape[1] // GRID_W
    tables = na_tables(rows)
    q_scale = NA_HEAD_DIM ** -0.5
    xc = ctx
    for l in range(DEPTH):
        last = l == DEPTH - 1
        sh1, sc1, gt1, sh2, sc2, gt2 = [m[:, None, :] for m in modulation(c, w_mod[l], b_mod[l])]
        csh1, csc1, cgt1, csh2, csc2, cgt2 = modulation(c_ctx, w_mod[l], b_mod[l])

        h = modulate(layer_norm(x), sh1, sc1)
        hc = modulate(layer_norm(xc), csh1, csc1)
        u = h @ w_in[l] + b_in[l]
        if last:
            uc_kv = hc @ w_in[l][:, OFF_K:] + b_in[l][OFF_K:]
            k_c, v_c = split_heads(uc_kv[..., :D_NA]), split_heads(uc_kv[..., D_NA:])
        else:
            uc = hc @ w_in[l] + b_in[l]
            k_c, v_c = split_heads(uc[..., OFF_K:OFF_V]), split_heads(uc[..., OFF_V:])

        ya = short_conv_mixer(u[..., OFF_A:OFF_B], w_short[l])
        yb = conformer_conv_mixer(u[..., OFF_B:OFF_Q], w_conf_dw[l], b_conf_dw[l], g_conf_ln[l], b_conf_ln[l])
        yc = neighbourhood_attention(split_heads(u[..., OFF_Q:OFF_K]) * q_scale, split_heads(u[..., OFF_K:OFF_V]),
                                     split_heads(u[..., OFF_V:]), k_c, v_c, na_rpb[l], tables)
        y = jnp.concatenate([ya, yb, yc], axis=-1) @ w_out[l] + b_out[l]
        x_mid = post_norm(x, y, gt1, g_post1[l], b_post1[l])

        if not last:
            yac = short_conv_mixer(uc[..., OFF_A:OFF_B], w_short[l])
            ybc = conformer_conv_mixer(uc[..., OFF_B:OFF_Q], w_conf_dw[l], b_conf_dw[l], g_conf_ln[l], b_conf_ln[l])
            ycc = context_attention(split_heads(uc[..., OFF_Q:OFF_K]) * q_scale, k_c, v_c)
            yctx = jnp.concatenate([yac, ybc, ycc], axis=-1) @ w_out[l] + b_out[l]
            xc_mid = post_norm(xc, yctx, cgt1, g_post1[l], b_post1[l])

        hm = modulate(layer_norm(x_mid), sh2, sc2)
        ym = expert_choice_ffn(hm, w_router[l], w_gate[l], w_up[l], w_down[l])
        x = post_norm(x_mid, ym, gt2, g_post2[l], b_post2[l])

        if not last:
            hmc = modulate(layer_norm(xc_mid), csh2, csc2)
            ymc = expert_choice_ffn(hmc, w_router[l], w_gate[l], w_up[l], w_down[l])
            xc = post_norm(xc_mid, ymc, cgt2, g_post2[l], b_post2[l])
    return x
```

```python
import contextlib
import numpy as np
import concourse.bass as bass
import concourse.mybir as mybir
from concourse.bass_utils import run_bass_kernel_spmd

F32 = mybir.dt.float32
BF16 = mybir.dt.bfloat16
I32 = mybir.dt.int32
AF = mybir.ActivationFunctionType
ALU = mybir.AluOpType
AX = mybir.AxisListType

ENGS = ["sync", "scalar", "gpsimd", "vector", "tensor"]
DMA_POOL = {"sync": 10, "scalar": 6, "gpsimd": 10}

D = 1024
S = 2048
CTX = 256
NT = S + CTX
NCH = NT // 128
DEPTH = 2
NE = 16
CAP = 256
CAPC = 32
LN_EPS = 1e-5
ALPHA = (2.0 * DEPTH) ** 0.25
NPAD = 96
PC_BMOD, PC_BIN, PC_WSH, PC_WCF, PC_BCF, PC_GLN, PC_BLN, NPC = 0, 16, 34, 40, 102, 104, 106, 108
PR_BMOD, PR_BV, PR_BOUT, PR_G1, PR_B1, PR_G2, PR_B2, NPR = 0, 4096, 4608, 5632, 6656, 7680, 8704, 9728
LOFF, COFF, HPW = 15, 15 + S + 30, 15 + S + 30 + CTX + 15


class _Op:
    __slots__ = ("id", "eng", "fn", "deps", "dma", "needed", "tick", "sem", "semval", "prev_same_sem", "multi")

    def __init__(self, id, eng, fn, dma):
        self.id = id
        self.eng = eng
        self.fn = fn
        self.dma = dma
        self.deps = {}
        self.needed = False
        self.tick = None
        self.sem = None
        self.semval = None
        self.prev_same_sem = None
        self.multi = False


class Prog:
    def __init__(self, nc):
        self.nc = nc
        self.ops = {e: [] for e in ENGS}
        self.n = 0
        self.last_writer = {}
        self.readers = {}
        self.bar = {e: {} for e in ENGS}
        self.since_bar = {}

    def op(self, eng, fn, reads=(), writes=(), dma=False, multi=False):
        o = _Op(self.n, eng, fn, dma)
        o.multi = multi
        self.n += 1
        if self.bar[eng]:
            o.deps.update(self.bar[eng])
            self.bar[eng] = {}
            o.multi = True
        for k in reads:
            w = self.last_writer.get(k)
            if w is not None:
                o.deps[w.id] = w
        for k in writes:
            w = self.last_writer.get(k)
            if w is not None:
                o.deps[w.id] = w
            rd = self.readers.get(k)
            if rd:
                for r in rd[0].values():
                    if r.eng == eng and not dma and eng == "tensor":
                        continue
                    o.deps[r.id] = r
                for r in rd[1]:
                    o.deps[r.id] = r
        for k in reads:
            rd = self.readers.setdefault(k, ({}, []))
            if dma:
                rd[1].append(o)
            else:
                rd[0][eng] = o
        for k in writes:
            self.last_writer[k] = o
            self.readers[k] = ({}, [])
        self.ops[eng].append(o)
        if dma:
            self.since_bar[o.id] = o
        else:
            self.since_bar[("c", eng)] = o
        return o

    def dma(self, eng, out, in_, reads=(), writes=(), **kw):
        return self.op(eng, lambda e: e.dma_start(out=out, in_=in_, **kw), reads, writes, dma=True)

    def barrier(self):
        snap = {o.id: o for o in self.since_bar.values()}
        for e in ENGS:
            self.bar[e].update(snap)
        self.since_bar = {}

    def emit(self):
        nc = self.nc
        for e in ENGS:
            for o in self.ops[e]:
                for d in o.deps.values():
                    if not d.dma:
                        if d.eng == "tensor" and o.eng == "tensor" and not o.dma:
                            continue
                        d.needed = True
        ticks = {}
        for e in ENGS:
            t = 0
            for o in self.ops[e]:
                if not o.dma and o.needed:
                    t += 1
                    o.tick = t
            ticks[e] = t
        stack = contextlib.ExitStack()
        esem = {e: stack.enter_context(nc.semaphore("e_" + e)) for e in ENGS}
        dcount = {}
        for e, npool in DMA_POOL.items():
            pool = [stack.enter_context(nc.semaphore("d_%s_%d" % (e, i))) for i in range(npool)]
            last = [None] * npool
            vals = [0] * npool
            i = 0
            for o in self.ops[e]:
                if o.dma:
                    j = i % npool
                    i += 1
                    vals[j] += 16
                    o.sem = pool[j]
                    o.semval = vals[j]
                    o.prev_same_sem = last[j]
                    last[j] = o
            dcount[e] = (pool, vals)
        block = stack.enter_context(nc.Block())

        def run_engine(ename, eh):
            waited = {}
            waited_seq = {}
            for o in self.ops[ename]:
                need = {}
                for d in o.deps.values():
                    if d.dma:
                        key = ("d", id(d.sem))
                        if need.get(key, (None, 0))[1] < d.semval:
                            need[key] = (d.sem, d.semval)
                    else:
                        if d.eng == "tensor" and ename == "tensor" and not o.dma:
                            continue
                        key = ("e", d.eng)
                        if need.get(key, (None, 0))[1] < d.tick:
                            need[key] = (esem[d.eng], d.tick)
                if o.dma and o.prev_same_sem is not None:
                    p = o.prev_same_sem
                    key = ("d", id(p.sem))
                    if need.get(key, (None, 0))[1] < p.semval:
                        need[key] = (p.sem, p.semval)
                todo = []
                ref = waited_seq if (o.dma or o.multi) else waited
                for key, (sem, val) in need.items():
                    if ref.get(key, 0) >= val:
                        continue
                    todo.append((key, sem, val))
                nstand = len(todo) if (o.multi or o.dma) else max(0, len(todo) - 1)
                for key, sem, val in todo[:nstand]:
                    eh.wait_ge(sem, val)
                    waited_seq[key] = max(waited_seq.get(key, 0), val)
                    waited[key] = max(waited.get(key, 0), val)
                ins = o.fn(eh)
                if nstand < len(todo):
                    key, sem, val = todo[-1]
                    ins._wait_ge(sem, val)
                    waited[key] = max(waited.get(key, 0), val)
                if o.dma:
                    ins.then_inc(o.sem, 16)
                elif o.needed:
                    ins.then_inc(esem[ename], 1)
            if ename == "gpsimd":
                for e2 in ENGS:
                    if ticks[e2] > 0 and e2 != "gpsimd":
                        eh.wait_ge(esem[e2], ticks[e2])
                for e2 in DMA_POOL:
                    pool, vals = dcount[e2]
                    for s, v in zip(pool, vals):
                        if v > 0:
                            eh.wait_ge(s, v)

        @block.sync
        def _(eh):
            run_engine("sync", eh)

        @block.scalar
        def _(eh):
            run_engine("scalar", eh)

        @block.vector
        def _(eh):
            run_engine("vector", eh)

        @block.tensor
        def _(eh):
            run_engine("tensor", eh)

        @block.gpsimd
        def _(eh):
            run_engine("gpsimd", eh)

        stack.close()


def interleave(chains, width, stagger=None):
    if not chains:
        return
    L = max(len(c) for c in chains)
    if stagger is None:
        stagger = max(1, L // width)
    active = []
    nxt = 0
    since = stagger
    while nxt < len(chains) or active:
        if len(active) < width and nxt < len(chains) and since >= stagger:
            active.append(iter(chains[nxt]))
            nxt += 1
            since = 0
        since += 1
        for it in list(active):
            th = next(it, None)
            if th is None:
                active.remove(it)
            else:
                th()


def build(stop_after=None, dumps=()):
    nc = bass.Bass("TRN2", target_bir_lowering=False)

    def din(name, shape, dt=F32):
        return nc.dram_tensor(name, list(shape), dt, kind="ExternalInput").ap()

    xin = din("xin", [NT, D])
    ccT_d = din("ccT", [128, 8, 2])
    w_mod = din("w_mod", [DEPTH, D, 6 * D])
    w_in = din("w_in", [DEPTH, D, 2816])
    w_out = din("w_out", [DEPTH, D, D])
    w_router = din("w_router", [DEPTH, D, NE])
    w_gate = din("w_gate", [DEPTH, NE, D, D])
    w_up = din("w_up", [DEPTH, NE, D, D])
    w_down = din("w_down", [DEPTH, NE, D, D])
    pcol_d = din("pcol", [128, DEPTH, NPC])
    prow_d = din("prow", [DEPTH, 128, NPR])
    rpbt_d = din("rpbt", [DEPTH, 128, 8, 896])
    cmask_d = din("cmask", [128, 896])
    ident_d = din("ident", [128, 128])
    iota_d = din("iota", [128, 256])
    tri_d = din("tri", [128, 128])
    tvcp_d = din("tvcp", [128, NCH, 2])
    dumoff_d = din("dumoff", [128, 1])
    out_d = nc.dram_tensor("out", [S, D], F32, kind="ExternalOutput").ap()
    xmid_d = nc.dram_tensor("xmid_s", [NT, D], F32).ap()
    xcur_d = nc.dram_tensor("xcur_s", [NT, D], F32).ap()
    hm_d = nc.dram_tensor("hm_s", [NT + NPAD, D], BF16).ap()
    acc_d = nc.dram_tensor("acc_s", [NT + NPAD, D], F32).ap()

    P = Prog(nc)
    dump_out = {}

    def dump(name, src_ap, shape, dt, key, dst_view=None):
        if name not in dumps:
            return
        t = nc.dram_tensor("dbg_" + name, list(shape), dt, kind="ExternalOutput").ap()
        dump_out[name] = t
        P.dma("sync", dst_view(t) if dst_view else t, src_ap, reads=key if isinstance(key, list) else [key])

    class Stop(Exception):
        pass

    def check_stop(tag):
        if stop_after == tag:
            raise Stop()

    V, A_, G, T_, SY = "vector", "scalar", "gpsimd", "tensor", "sync"

    def body():
      with contextlib.ExitStack() as g0:
        uid = [0]

        def sb(st, name, shape, dt=F32):
            uid[0] += 1
            return st.enter_context(nc.sbuf_tensor("%s_%d" % (name, uid[0]), list(shape), dt))

        def pst(st, name, shape, dt=F32):
            uid[0] += 1
            return st.enter_context(nc.psum_tensor("%s_%d" % (name, uid[0]), list(shape), dt))

        ident_f = sb(g0, "ident_f", [128, 128])
        ident_b = sb(g0, "ident_b", [128, 128], BF16)
        ones_b = sb(g0, "ones_b", [128, 128], BF16)
        ones_f = sb(g0, "ones_f", [128, 128])
        tri_f = sb(g0, "tri_f", [128, 128])
        tri_b = sb(g0, "tri_b", [128, 128], BF16)
        iota_f = sb(g0, "iota_f", [128, 256])
        tvcp_f = sb(g0, "tvcp_f", [128, NCH, 2])
        dumoff = sb(g0, "dumoff", [128, 1])
        pcol = sb(g0, "pcol", [128, DEPTH, NPC])
        ccT = sb(g0, "ccT", [128, 8, 2])
        scT_f = sb(g0, "scT_f", [128, 8, 2])
        scT_b = sb(g0, "scT_b", [128, 8, 2], BF16)
        screp = [sb(g0, "screp%d" % j, [128, 8, 128], BF16) for j in range(2)]
        neghalf = sb(g0, "neghalf", [128, 32])
        P.dma(SY, ident_f[:], ident_d, writes=["ident_f"])
        P.dma(SY, tri_f[:], tri_d, writes=["tri_f"])
        P.dma(SY, iota_f[:], iota_d, writes=["iota_f"])
        P.dma(SY, tvcp_f[:], tvcp_d, writes=["tvcp_f"])
        P.dma(SY, dumoff[:], dumoff_d, writes=["dumoff"])
        P.dma(SY, pcol[:], pcol_d, writes=["pcol"])
        P.dma(SY, ccT[:], ccT_d, writes=["ccT"])
        P.op(V, lambda e: e.tensor_copy(out=ident_b[:], in_=ident_f[:]), ["ident_f"], ["ident_b"])
        P.op(V, lambda e: e.tensor_copy(out=tri_b[:], in_=tri_f[:]), ["tri_f"], ["tri_b"])
        P.op(V, lambda e: e.memset(ones_b[:], 1.0), [], ["ones_b"])
        P.op(V, lambda e: e.memset(ones_f[:], 1.0), [], ["ones_f"])
        P.op(V, lambda e: e.memset(neghalf[:], -0.5), [], ["neghalf"])
        P.op(A_, lambda e: e.activation(out=scT_f[:], in_=ccT[:], func=AF.Silu), ["ccT"], ["scT_f"])
        P.op(V, lambda e: e.tensor_copy(out=scT_b[:], in_=scT_f[:]), ["scT_f"], ["scT_b"])
        for j in range(2):
            for kc in range(8):
                P.op(V, lambda e, j=j, kc=kc: e.tensor_scalar(out=screp[j][:, kc, :], in0=ones_f[:], scalar1=scT_f[:, kc, j:j + 1],
                                                              scalar2=None, op0=ALU.mult), ["ones_f", "scT_f"], [("screp", j)])
        with contextlib.ExitStack() as sz:
            hz = sb(sz, "hz", [128, D], BF16)
            zero_t = sb(sz, "zero_t", [128, D])
            P.op(V, lambda e: e.memset(zero_t[:], 0.0), [], ["zero_t"])
            P.dma(SY, acc_d[NT:NT + NPAD, :], zero_t[0:NPAD, :], reads=["zero_t"], writes=["acc_pad"])
            P.op(V, lambda e: e.memset(hz[:], 0.0), [], ["hz"])
            P.dma(SY, hm_d[NT:NT + NPAD, :], hz[0:NPAD, :], reads=["hz"], writes=["hm_pad"])
        P.barrier()

        def pc(l, c):
            return pcol[:, l, c:c + 1]

        def ln_stats(e_key, x_ap, st6, mv, keys_r, keys_w):
            P.op(V, lambda e: e.bn_stats(out=st6[:, 0, :], in_=x_ap[:, 0:512]), keys_r, [e_key])
            P.op(V, lambda e: e.bn_stats(out=st6[:, 1, :], in_=x_ap[:, 512:1024]), keys_r, [e_key])
            P.op(V, lambda e: e.bn_aggr(out=mv, in_=st6[:].rearrange("p a b -> p (a b)")), [e_key], keys_w)

        def rstd_nb(mean_ap, var_ap, ve, rstd, nb, n, kr, kw):
            P.op(V, lambda e: e.tensor_scalar(out=ve, in0=var_ap, scalar1=LN_EPS, scalar2=None, op0=ALU.add), kr, [kw + "_ve"])
            P.op(G, lambda e: e.tensor_tensor(out=rstd, in0=ve, in1=neghalf[:, 0:n], op=ALU.pow), [kw + "_ve", "neghalf"], [kw + "_rstd"])
            P.op(V, lambda e: e.scalar_tensor_tensor(out=nb, in0=mean_ap, scalar=-1.0, in1=rstd, op0=ALU.mult, op1=ALU.mult),
                 kr + [kw + "_rstd"], [kw + "_nb"])

        def layer(l):
          last = l == DEPTH - 1
          xsrc = xin if l == 0 else xcur_d
          nch_out = 16 if last else NCH
          with contextlib.ExitStack() as gl:
            gt2p = [sb(gl, "gt2p%d" % j, [128, D]) for j in range(2)]
            modT = sb(gl, "modT", [128, 16, 2])
            affT = sb(gl, "affT", [16, NT])
            aff_all = sb(gl, "aff_all", [128, NCH, NE])
            sy = contextlib.ExitStack()
            ycatT = sb(sy, "ycatT", [128, 8, NT], BF16)

            with contextlib.ExitStack() as s1:
                wm = [sb(s1, "wm%d" % i, [128, 8, 512], BF16) for i in range(4)]
                pm = [pst(s1, "pm%d" % i, [128, 512]) for i in range(2)]
                for g in range(4):
                    P.dma(G, wm[g % 4][:], w_mod[l][:, g * 512:(g + 1) * 512].rearrange("(kc p) n -> p kc n", p=128),
                          writes=[("wm", g % 4)])
                    for c4 in range(4):
                        cc = g * 4 + c4
                        for kc in range(8):
                            P.op(T_, lambda e, g=g, c4=c4, kc=kc, cc=cc: e.matmul(pm[cc % 2][:, 0:2], lhsT=wm[g % 4][:, kc, c4 * 128:(c4 + 1) * 128],
                                                                                 rhs=scT_b[:, kc, :], start=(kc == 0), stop=(kc == 7)),
                                 [("wm", g % 4), "scT_b"], [("pm", cc % 2)])
                        if cc >= 8:
                            P.op(V, lambda e, cc=cc: e.tensor_scalar(out=modT[:, cc, :], in0=pm[cc % 2][:, 0:2], scalar1=pc(l, PC_BMOD + cc),
                                                                     scalar2=1.0, op0=ALU.add, op1=ALU.add), [("pm", cc % 2), "pcol"], ["modT"])
                        else:
                            P.op(V, lambda e, cc=cc: e.tensor_scalar(out=modT[:, cc, :], in0=pm[cc % 2][:, 0:2], scalar1=pc(l, PC_BMOD + cc),
                                                                     scalar2=None, op0=ALU.add), [("pm", cc % 2), "pcol"], ["modT"])
            P.barrier()
            dump("modT%d" % l, modT[:], [128, 16, 2], F32, "modT")
            check_stop("M1_%d" % l)

            with contextlib.ExitStack() as smix:
              hT = sb(smix, "hT", [128, 8, NT], BF16)
              wt = [sb(smix, "wt%d" % i, [128, 8, 512], BF16) for i in range(3)]

              def load_w(i, c0, ncols):
                  P.dma(G, wt[i][:, :, 0:ncols], w_in[l][:, c0:c0 + ncols].rearrange("(kc p) n -> p kc n", p=128), writes=[("wt", i)])

              load_w(0, 0, 512)
              load_w(1, 512, 256)
              load_w(2, 768, 512)
              with contextlib.ExitStack() as sa:
                xg = [sb(sa, "xg%d" % i, [128, 4, D]) for i in range(3)]
                st6a = [sb(sa, "st6a%d" % i, [128, 4, 2, 6]) for i in range(3)]
                mva = [sb(sa, "mva%d" % i, [128, 4, 2]) for i in range(3)]
                vea = [sb(sa, "vea%d" % i, [128, 4]) for i in range(3)]
                lna = [sb(sa, "lna%d" % i, [128, 4]) for i in range(3)]
                rstda = [sb(sa, "rstda%d" % i, [128, 4]) for i in range(3)]
                nba = [sb(sa, "nba%d" % i, [128, 4]) for i in range(3)]
                pa = [pst(sa, "pa%d" % i, [128, 512]) for i in range(4)]
                pcount = [0]

                def a_chain(tg):
                    ncg = 4 if tg < 4 else 2
                    j = 0 if tg < 4 else 1
                    b = tg % 3
                    ch = []

                    def a0():
                        P.dma(SY, xg[b][:, 0:ncg, :], xsrc[tg * 512:tg * 512 + ncg * 128, :].rearrange("(c p) d -> p c d", p=128),
                              reads=[("xcur", tg)], writes=[("xg", b, c) for c in range(ncg)])
                    ch.append(a0)
                    for c in range(ncg):
                        def a1(c=c):
                            P.op(V, lambda e: e.bn_stats(out=st6a[b][:, c, 0, :], in_=xg[b][:, c, 0:512]), [("xg", b, c)], [("st6a", b, c)])
                            P.op(V, lambda e: e.bn_stats(out=st6a[b][:, c, 1, :], in_=xg[b][:, c, 512:1024]), [("xg", b, c)], [("st6a", b, c)])
                            P.op(V, lambda e: e.bn_aggr(out=mva[b][:, c, :], in_=st6a[b][:, c, :, :].rearrange("p a b -> p (a b)")), [("st6a", b, c)], [("mva", b)])
                        ch.append(a1)

                    def a2():
                        P.op(V, lambda e: e.tensor_scalar(out=vea[b][:, 0:ncg], in0=mva[b][:, 0:ncg, 1], scalar1=LN_EPS, scalar2=None, op0=ALU.add), [("mva", b)], [("vea", b)])
                        P.op(A_, lambda e: e.activation(out=lna[b][:, 0:ncg], in_=vea[b][:, 0:ncg], func=AF.Ln), [("vea", b)], [("lna", b)])
                        P.op(A_, lambda e: e.activation(out=rstda[b][:, 0:ncg], in_=lna[b][:, 0:ncg], func=AF.Exp, scale=-0.5), [("lna", b)], [("rstda", b)])
                        P.op(V, lambda e: e.scalar_tensor_tensor(out=nba[b][:, 0:ncg], in0=mva[b][:, 0:ncg, 0], scalar=-1.0, in1=rstda[b][:, 0:ncg], op0=ALU.mult, op1=ALU.mult),
                             [("mva", b), ("rstda", b)], [("nba", b)])
                    ch.append(a2)
                    for c in range(ncg):
                        def a3(c=c):
                            P.op(A_, lambda e: e.activation(out=xg[b][:, c, :], in_=xg[b][:, c, :], func=AF.Identity, bias=nba[b][:, c:c + 1], scale=rstda[b][:, c:c + 1]),
                                 [("xg", b, c), ("rstda", b), ("nba", b)], [("xg", b, c)])
                        ch.append(a3)
                    for kc in range(8):
                        def a4(kc=kc):
                            pb = pcount[0] % 4
                            pcount[0] += 1
                            for c in range(ncg):
                                P.op(T_, lambda e, c=c: e.transpose(out=pa[pb][:, c * 128:(c + 1) * 128], in_=xg[b][:, c, kc * 128:(kc + 1) * 128], identity=ident_f[:]),
                                     [("xg", b, c), "ident_f"], [("pa", pb)])
                            if kc % 2 == 0:
                                P.op(V, lambda e: e.tensor_scalar(out=hT[:, kc, tg * 512:tg * 512 + ncg * 128], in0=pa[pb][:, 0:ncg * 128],
                                                                  scalar1=modT[:, 8 + kc, j:j + 1], scalar2=modT[:, kc, j:j + 1], op0=ALU.mult, op1=ALU.add),
                                     [("pa", pb), "modT"], [("hT", tg)])
                            else:
                                P.op(A_, lambda e: e.activation(out=hT[:, kc, tg * 512:tg * 512 + ncg * 128], in_=pa[pb][:, 0:ncg * 128], func=AF.Identity,
                                                                bias=modT[:, kc, j:j + 1], scale=modT[:, 8 + kc, j:j + 1]), [("pa", pb), "modT"], [("hT", tg)])
                        ch.append(a4)
                    return ch

                interleave([a_chain(tg) for tg in range(5)], 3, stagger=5)
              P.barrier()
              HT_ALL = [("hT", tg) for tg in range(5)]
              dump("hT%d" % l, hT[:], [128, 8, NT], BF16, ("hT", 0))
              check_stop("A_%d" % l)

              def proj_fm(ps_ap, wti, lc, nt, n, pkey):
                  for kc in range(8):
                      P.op(T_, lambda e, kc=kc: e.matmul(ps_ap, lhsT=wt[wti][:, kc, lc * 128:(lc + 1) * 128], rhs=hT[:, kc, nt * 512:nt * 512 + n],
                                                         start=(kc == 0), stop=(kc == 7)), [("wt", wti)] + HT_ALL, [pkey])

              ntl = [(nt, 512) for nt in range(4)] + [(4, 256)]

              with contextlib.ExitStack() as sb1:
                cgs = sb(sb1, "cgs", [128, NT + 4])
                zp = sb(sb1, "zp", [128, NT + 4])
                acc = sb(sb1, "acc", [128, NT + 4])
                pb1 = [pst(sb1, "pb1_%d" % i, [128, 512]) for i in range(4)]
                P.op(G, lambda e: e.memset(zp[:], 0.0), [], ["zp"])
                P.op(G, lambda e: e.memset(acc[:], 0.0), [], ["acc"])

                def pcs(nt):
                    return 1 + nt * 512 if nt < 4 else 2051
                ib = 0
                for j in range(2):
                    for nt, n in ntl:
                        pb = ib % 4; ib += 1
                        proj_fm(pb1[pb][:, 0:n], 0, 2 + j, nt, n, ("pb1", pb))
                        P.op(A_, lambda e, pb=pb, nt=nt, n=n, j=j: e.activation(out=cgs[:, pcs(nt):pcs(nt) + n], in_=pb1[pb][:, 0:n], func=AF.Identity,
                                                                              bias=pc(l, PC_BIN + 2 + j), scale=1.0), [("pb1", pb), "pcol"], ["cgs"])
                        pb = ib % 4; ib += 1
                        proj_fm(pb1[pb][:, 0:n], 1, j, nt, n, ("pb1", pb))
                        P.op(V, lambda e, pb=pb, nt=nt, n=n, j=j: e.scalar_tensor_tensor(out=zp[:, pcs(nt):pcs(nt) + n], in0=pb1[pb][:, 0:n], scalar=pc(l, PC_BIN + 4 + j),
                                                                                       in1=cgs[:, pcs(nt):pcs(nt) + n], op0=ALU.add, op1=ALU.mult),
                             [("pb1", pb), "pcol", "cgs"], ["zp"])
                    W = NT + 2
                    P.op(V, lambda e, j=j: e.tensor_scalar(out=acc[:, 1:1 + W], in0=zp[:, 0:W], scalar1=pc(l, PC_WSH + j * 3 + 0), scalar2=None, op0=ALU.mult),
                         ["zp", "pcol"], ["acc"])
                    P.op(V, lambda e, j=j: e.scalar_tensor_tensor(out=acc[:, 1:1 + W], in0=zp[:, 1:1 + W], scalar=pc(l, PC_WSH + j * 3 + 1), in1=acc[:, 1:1 + W],
                                                                  op0=ALU.mult, op1=ALU.add), ["zp", "pcol", "acc"], ["acc"])
                    P.op(V, lambda e, j=j: e.scalar_tensor_tensor(out=acc[:, 1:1 + W], in0=zp[:, 2:2 + W], scalar=pc(l, PC_WSH + j * 3 + 2), in1=acc[:, 1:1 + W],
                                                                  op0=ALU.mult, op1=ALU.add), ["zp", "pcol", "acc"], ["acc"])
                    for nt, n in ntl:
                        pb = ib % 4; ib += 1
                        proj_fm(pb1[pb][:, 0:n], 0, j, nt, n, ("pb1", pb))
                        P.op(V, lambda e, pb=pb, nt=nt, n=n, j=j: e.scalar_tensor_tensor(out=ycatT[:, j, nt * 512:nt * 512 + n], in0=pb1[pb][:, 0:n], scalar=pc(l, PC_BIN + j),
                                                                                       in1=acc[:, pcs(nt):pcs(nt) + n], op0=ALU.add, op1=ALU.mult),
                             [("pb1", pb), "pcol", "acc"], [("ycatT", j)])
              P.barrier()
              check_stop("B1_%d" % l)

              with contextlib.ExitStack() as sb2:
                hp = [sb(sb2, "hp%d" % j, [128, HPW], BF16) for j in range(2)]
                Dm = sb(sb2, "Dm", [128, 62, 128], BF16)
                cv = [sb(sb2, "cv%d" % j, [128, NT]) for j in range(2)]
                cvT = sb(sb2, "cvT", [128, NCH, 256])
                sgt2 = [sb(sb2, "sgt%d" % i, [128, 512]) for i in range(2)]
                st2 = sb(sb2, "st2", [128, NCH, 6])
                mv2 = sb(sb2, "mv2", [128, NCH, 2])
                ve2 = sb(sb2, "ve2", [128, NCH])
                rstd2 = sb(sb2, "rstd2", [128, NCH])
                nb2 = sb(sb2, "nb2", [128, NCH])
                pb2 = [pst(sb2, "pb2_%d" % i, [128, 512]) for i in range(6)]
                load_w(0, 1280, 512)
                load_w(1, 1792, 512)
                for j in range(2):
                    P.op(G, lambda e, j=j: e.memset(hp[j][:], 0.0), [], [("hp", j)])
                for i in range(62):
                    P.op(V, lambda e, i=i: e.tensor_scalar(out=Dm[:, i, :], in0=ident_f[:], scalar1=pc(l, PC_WCF + i), scalar2=None, op0=ALU.mult),
                         ["ident_f", "pcol"], ["Dm"])

                def hoff(nt):
                    return LOFF + nt * 512 if nt < 4 else COFF
                ib = 0
                for j in range(2):
                    for nt, n in ntl:
                        pb = ib % 6; ib += 1
                        proj_fm(pb2[pb][:, 0:n], 2, 2 + j, nt, n, ("pb2", pb))
                        sgi = ib % 2
                        P.op(A_, lambda e, pb=pb, n=n, j=j, sgi=sgi: e.activation(out=sgt2[sgi][:, 0:n], in_=pb2[pb][:, 0:n], func=AF.Sigmoid,
                                                                                bias=pc(l, PC_BIN + 8 + j), scale=1.0), [("pb2", pb), "pcol"], [("sgt", sgi)])
                        pb = ib % 6; ib += 1
                        proj_fm(pb2[pb][:, 0:n], 2, j, nt, n, ("pb2", pb))
                        P.op(V, lambda e, pb=pb, nt=nt, n=n, j=j, sgi=sgi: e.scalar_tensor_tensor(out=hp[j][:, hoff(nt):hoff(nt) + n], in0=pb2[pb][:, 0:n],
                                                                                                scalar=pc(l, PC_BIN + 6 + j), in1=sgt2[sgi][:, 0:n], op0=ALU.add, op1=ALU.mult),
                             [("pb2", pb), "pcol", ("sgt", sgi)], [("hp", j)])
                for j in range(2):
                    for nt, n in ntl:
                        pb = ib % 6; ib += 1
                        base = hoff(nt) - 15
                        for k in range(31):
                            P.op(T_, lambda e, pb=pb, j=j, k=k, base=base, n=n: e.matmul(pb2[pb][:, 0:n], lhsT=Dm[:, j * 31 + k, :], rhs=hp[j][:, base + k:base + k + n],
                                                                                       start=(k == 0), stop=(k == 30)), ["Dm", ("hp", j)], [("pb2", pb)])
                        P.op(A_, lambda e, pb=pb, j=j, nt=nt, n=n: e.activation(out=cv[j][:, nt * 512:nt * 512 + n], in_=pb2[pb][:, 0:n], func=AF.Identity,
                                                                              bias=pc(l, PC_BCF + j), scale=1.0), [("pb2", pb), "pcol"], [("cv", j)])
                for tcn in range(NCH):
                    pb = ib % 6; ib += 1
                    for j in range(2):
                        P.op(T_, lambda e, pb=pb, j=j, tcn=tcn: e.transpose(out=pb2[pb][:, j * 128:(j + 1) * 128], in_=cv[j][:, tcn * 128:(tcn + 1) * 128], identity=ident_f[:]),
                             [("cv", j), "ident_f"], [("pb2", pb)])
                    P.op(V, lambda e, pb=pb, tcn=tcn: e.tensor_copy(out=cvT[:, tcn, :], in_=pb2[pb][:, 0:256]), [("pb2", pb)], [("cvT", tcn)])
                    P.op(V, lambda e, tcn=tcn: e.bn_stats(out=st2[:, tcn, :], in_=cvT[:, tcn, :]), [("cvT", tcn)], [("st2", tcn)])
                    P.op(V, lambda e, tcn=tcn: e.bn_aggr(out=mv2[:, tcn, :], in_=st2[:, tcn, :]), [("st2", tcn)], ["mv2"])
                rstd_nb(mv2[:, :, 0], mv2[:, :, 1], ve2[:], rstd2[:], nb2[:], NCH, ["mv2"], "B2")
                for tcn in range(NCH):
                    P.op(A_, lambda e, tcn=tcn: e.activation(out=cvT[:, tcn, :], in_=cvT[:, tcn, :], func=AF.Identity, bias=nb2[:, tcn:tcn + 1], scale=rstd2[:, tcn:tcn + 1]),
                         [("cvT", tcn), "B2_rstd", "B2_nb"], [("cvT", tcn)])
                for g4 in range(5):
                    ncg = 4 if g4 < 4 else 2
                    for j in range(2):
                        pb = ib % 6; ib += 1
                        for c in range(ncg):
                            tcn = g4 * 4 + c
                            P.op(T_, lambda e, pb=pb, c=c, j=j, tcn=tcn: e.transpose(out=pb2[pb][:, c * 128:(c + 1) * 128], in_=cvT[:, tcn, j * 128:(j + 1) * 128], identity=ident_f[:]),
                                 [("cvT", tcn), "ident_f"], [("pb2", pb)])
                        P.op(A_, lambda e, pb=pb, j=j, g4=g4, ncg=ncg: e.activation(out=ycatT[:, 2 + j, g4 * 512:g4 * 512 + ncg * 128], in_=pb2[pb][:, 0:ncg * 128], func=AF.Silu,
                                                                                  bias=pc(l, PC_BLN + j), scale=pc(l, PC_GLN + j)), [("pb2", pb), "pcol"], [("ycatT", 2 + j)])
              P.barrier()
              check_stop("B2_%d" % l)

              with contextlib.ExitStack() as sb3:
                vaug = sb(sb3, "vaug", [128, NCH, 8, 65], BF16)
                bv_r = sb(sb3, "bv_r", [128, 512])
                cm = sb(sb3, "cm", [128, 896])
                qT = [sb(sb3, "qT%d" % i, [128, NT], BF16) for i in range(2)]
                kT = [sb(sb3, "kT%d" % i, [128, NT], BF16) for i in range(2)]
                tb = [sb(sb3, "tb%d" % i, [128, 2, 896]) for i in range(2)]
                bq8 = sb(sb3, "bq8", [128, 4])
                sS = [sb(sb3, "sS%d" % i, [128, 640]) for i in range(3)]
                PT = [sb(sb3, "PT%d" % i, [128, 896], BF16) for i in range(3)]
                rsb = [sb(sb3, "rsb%d" % i, [128, 1]) for i in range(3)]
                ychp = sb(sb3, "ychp", [128, NCH, 128])
                NU = 3
                pAb = [pst(sb3, "pAb%d" % i, [128, 512]) for i in range(NU)]
                pBb = [pst(sb3, "pBb%d" % i, [128, 512]) for i in range(NU)]
                pO = [pBb[i][:, 384:449] for i in range(NU)]
                pG = [pst(sb3, "pG%d" % i, [128, 512]) for i in range(2)]
                load_w(2, 2304, 512)
                P.dma(A_, bv_r[:], prow_d[l][:, PR_BV:PR_BV + 512], writes=["bv_r"])
                P.dma(A_, cm[:], cmask_d, writes=["cm"])
                P.op(G, lambda e: e.memset(vaug[:], 1.0), [], ["vaug"])
                P.op(V, lambda e: e.tensor_scalar(out=bq8[:], in0=pcol[:, l, PC_BIN + 10:PC_BIN + 14], scalar1=0.125, scalar2=None, op0=ALU.mult), ["pcol"], ["bq8"])
                ib = 0
                for tcn in range(NCH):
                    pb = ib % 2; ib += 1
                    for kc in range(8):
                        P.op(T_, lambda e, pb=pb, kc=kc, tcn=tcn: e.matmul(pG[pb][:, 0:512], lhsT=hT[:, kc, tcn * 128:(tcn + 1) * 128], rhs=wt[2][:, kc, 0:512],
                                                                         start=(kc == 0), stop=(kc == 7)), [("wt", 2)] + HT_ALL, [("pG", pb)])
                    P.op(V, lambda e, pb=pb, tcn=tcn: e.tensor_tensor(out=vaug[:, tcn, :, 0:64], in0=pG[pb][:, 0:512].rearrange("p (h d) -> p h d", h=8),
                                                                    in1=bv_r[:].rearrange("p (h d) -> p h d", h=8), op=ALU.add), [("pG", pb), "bv_r"], ["vaug"])
                nqt = 16 if last else NCH
                u = 0
                for hpi in range(4):
                    qb = hpi % 2
                    for nt, n in ntl:
                        if last and nt == 4:
                            continue
                        pb = ib % 2; ib += 1
                        proj_fm(pG[pb][:, 0:n], 0, hpi, nt, n, ("pG", pb))
                        P.op(A_, lambda e, pb=pb, nt=nt, n=n, qb=qb, hpi=hpi: e.activation(out=qT[qb][:, nt * 512:nt * 512 + n], in_=pG[pb][:, 0:n], func=AF.Identity,
                                                                                         bias=bq8[:, hpi:hpi + 1], scale=0.125), [("pG", pb), "bq8"], [("qT", qb)])
                    for nt, n in ntl:
                        pb = ib % 2; ib += 1
                        proj_fm(pG[pb][:, 0:n], 1, hpi, nt, n, ("pG", pb))
                        P.op(A_, lambda e, pb=pb, nt=nt, n=n, qb=qb, hpi=hpi: e.activation(out=kT[qb][:, nt * 512:nt * 512 + n], in_=pG[pb][:, 0:n], func=AF.Identity,
                                                                                         bias=pc(l, PC_BIN + 14 + hpi), scale=1.0), [("pG", pb), "pcol"], [("kT", qb)])
                    P.dma(SY, tb[qb][:], rpbt_d[l][:, 2 * hpi:2 * hpi + 2, :], writes=[("tb", qb)])
                    for hh in range(2):
                        P.op(V, lambda e, qb=qb, hh=hh: e.tensor_tensor(out=tb[qb][:, hh, :], in0=tb[qb][:, hh, :], in1=cm[:], op=ALU.add), [("tb", qb), "cm"], [("tb", qb)])
                    def unit_chain(hh, qt, ub, qb=qb, hpi=hpi):
                        h = 2 * hpi + hh
                        r0, r1 = hh * 64, hh * 64 + 64
                        if qt < 16:
                            r = 2 * qt
                            if r <= 2:
                                krs = [6, 4, 2, 0]
                            elif r >= 28:
                                krs = [30, 28, 26, 24]
                            else:
                                krs = [r + 4, r + 2, r, r - 2, r - 4]
                            interior = 4 <= r <= 26
                            s0 = (6 - (krs[0] - r)) * 64
                            btc = [kr // 2 for kr in krs]
                        else:
                            interior = False
                            s0 = 0
                            btc = []
                        nbnd = len(btc)
                        nA = min(nbnd, 4)
                        nBb = nbnd - nA
                        tcsB = btc[nA:] + [16, 17]
                        tcs = btc + [16, 17]
                        nck = len(tcs)
                        kA, kB = ("pAb", ub), ("pBb", ub)
                        ch = []

                        def mmS(dst, i, tck):
                            P.op(T_, lambda e: e.matmul(dst, lhsT=kT[qb][r0:r1, tck * 128:(tck + 1) * 128], rhs=qT[qb][r0:r1, qt * 128:(qt + 1) * 128],
                                                        start=True, stop=True), [("kT", qb), ("qT", qb)], [kA if i < nA else kB])
                        if nA:
                            def s1a():
                                for i in range(nA):
                                    mmS(pAb[ub][:, i * 128:(i + 1) * 128], i, btc[i])
                            ch.append(s1a)

                        def s1b():
                            for i2, tck in enumerate(tcsB):
                                mmS(pBb[ub][:, i2 * 128:(i2 + 1) * 128], nA + i2, tck)
                        ch.append(s1b)
                        if nA:
                            def s2a():
                                P.op(V, lambda e: e.tensor_tensor(out=sS[ub][:, 0:nA * 128], in0=pAb[ub][:, 0:nA * 128],
                                                                  in1=tb[qb][:, hh, s0:s0 + nA * 128], op=ALU.add), [kA, ("tb", qb)], [("sSa", ub)])
                                if interior:
                                    P.op(V, lambda e: e.memset(sS[ub][0:64, 0:64], -1e30), [], [("sSa", ub)])
                            ch.append(s2a)
                        if nBb:
                            def s2b():
                                P.op(V, lambda e: e.tensor_tensor(out=sS[ub][:, 512:640], in0=pBb[ub][:, 0:128],
                                                                  in1=tb[qb][:, hh, s0 + 512:s0 + 640], op=ALU.add), [kB, ("tb", qb)], [("sSb", ub)])
                                P.op(V, lambda e: e.memset(sS[ub][0:64, 512 + 64:640], -1e30), [], [("sSb", ub)])
                            ch.append(s2b)
                        if nA:
                            def s3a():
                                P.op(A_, lambda e: e.activation(out=PT[ub][:, 0:nA * 128], in_=sS[ub][:, 0:nA * 128], func=AF.Exp), [("sSa", ub)], [("PTa", ub)])
                            ch.append(s3a)

                        def s3b():
                            if nBb:
                                P.op(A_, lambda e: e.activation(out=PT[ub][:, 512:640], in_=sS[ub][:, 512:640], func=AF.Exp), [("sSb", ub)], [("PTb", ub)])
                            P.op(A_, lambda e: e.activation(out=PT[ub][:, nbnd * 128:nbnd * 128 + 256], in_=pBb[ub][:, nBb * 128:nBb * 128 + 256], func=AF.Exp),
                                 [kB], [("PTb", ub)])
                        ch.append(s3b)

                        def s4():
                            for i, tck in enumerate(tcs):
                                kr1 = 64 if (interior and i == 0) else 128
                                P.op(T_, lambda e, i=i, tck=tck, kr1=kr1: e.matmul(
                                    pO[ub], lhsT=PT[ub][0:kr1, i * 128:(i + 1) * 128], rhs=vaug[0:kr1, tck, h, :],
                                    start=(i == 0), stop=(i == nck - 1)), [("PTa", ub), ("PTb", ub), "vaug"], [kB])
                        ch.append(s4)

                        def s5():
                            P.op(V, lambda e: e.reciprocal(out=rsb[ub][:], in_=pO[ub][:, 64:65]), [kB], [("rsb", ub)])
                            P.op(V, lambda e: e.tensor_scalar(out=ychp[:, qt, hh * 64:(hh + 1) * 64], in0=pO[ub][:, 0:64], scalar1=rsb[ub][:],
                                                              scalar2=None, op0=ALU.mult), [kB, ("rsb", ub)], [("ychp", qt)])
                        ch.append(s5)
                        return ch

                    chains = []
                    for hh in range(2):
                        for qt in range(nqt):
                            chains.append(unit_chain(hh, qt, u % NU))
                            u += 1
                    interleave(chains, NU, stagger=3)
                    for g4 in range(5):
                        ncg = 4 if g4 < 4 else 2
                        if last and g4 == 4:
                            continue
                        pb = ib % 2; ib += 1
                        for c in range(ncg):
                            qt = g4 * 4 + c
                            P.op(T_, lambda e, pb=pb, c=c, qt=qt: e.transpose(out=pG[pb][:, c * 128:(c + 1) * 128], in_=ychp[:, qt, :], identity=ident_f[:]),
                                 [("ychp", qt), "ident_f"], [("pG", pb)])
                        P.op(A_, lambda e, pb=pb, g4=g4, ncg=ncg, hpi=hpi: e.activation(out=ycatT[:, 4 + hpi, g4 * 512:g4 * 512 + ncg * 128], in_=pG[pb][:, 0:ncg * 128], func=AF.Identity,
                                                                                      bias=0.0, scale=1.0), [("pG", pb)], [("ycatT", 4 + hpi)])
              P.barrier()
            P.barrier()
            YC_ALL = [("ycatT", i) for i in range(8)]
            dump("ycatT%d" % l, ycatT[:], [128, 8, NT], BF16, ("ycatT", 0))
            check_stop("B3_%d" % l)

            with contextlib.ExitStack() as sc:
              gt1p = [sb(sc, "gt1p%d" % j, [128, D]) for j in range(2)]
              sh2r = [sb(sc, "sh2r%d" % j, [128, D]) for j in range(2)]
              sc2p = [sb(sc, "sc2p%d" % j, [128, D]) for j in range(2)]
              rows = {0: gt1p, 1: sh2r, 2: sc2p, 3: gt2p}
              with contextlib.ExitStack() as sm2:
                wm2 = [sb(sm2, "wm2_%d" % i, [128, 8, 512], BF16) for i in range(4)]
                bmr = sb(sm2, "bmr", [128, 4096])
                pm2 = [pst(sm2, "pm2_%d" % i, [128, 512]) for i in range(2)]
                P.dma(A_, bmr[:], prow_d[l][:, PR_BMOD:PR_BMOD + 4096], writes=["bmr"])
                ib = 0
                for g in range(8):
                    P.dma(G, wm2[g % 4][:], w_mod[l][:, 2048 + g * 512:2048 + (g + 1) * 512].rearrange("(kc p) n -> p kc n", p=128), writes=[("wm2", g % 4)])
                    which, half = g // 2, g % 2
                    for j in range(2):
                        if last and j == 1:
                            continue
                        pb = ib % 2; ib += 1
                        for kc in range(8):
                            P.op(T_, lambda e, pb=pb, kc=kc, j=j, g=g: e.matmul(pm2[pb][:, 0:512], lhsT=screp[j][:, kc, :], rhs=wm2[g % 4][:, kc, :], start=(kc == 0), stop=(kc == 7)),
                                 [("screp", j), ("wm2", g % 4)], [("pm2", pb)])
                        dst = rows[which][j][:, half * 512:(half + 1) * 512]
                        if which == 1:
                            P.op(V, lambda e, pb=pb, dst=dst, g=g: e.tensor_tensor(out=dst, in0=pm2[pb][:, 0:512], in1=bmr[:, g * 512:(g + 1) * 512], op=ALU.add),
                                 [("pm2", pb), "bmr"], [("rows", which, j)])
                        else:
                            P.op(V, lambda e, pb=pb, dst=dst, g=g: e.scalar_tensor_tensor(out=dst, in0=pm2[pb][:, 0:512], scalar=1.0, in1=bmr[:, g * 512:(g + 1) * 512],
                                                                                       op0=ALU.add, op1=ALU.add), [("pm2", pb), "bmr"], [("rows", which, j)])
              P.barrier()
              check_stop("M2_%d" % l)

              with contextlib.ExitStack() as sc1:
                NC3 = 3
                wout = sb(sc1, "wout", [128, 8, D], BF16)
                bout_r = sb(sc1, "bout_r", [128, D])
                g1_r = sb(sc1, "g1_r", [128, D])
                b1_r = sb(sc1, "b1_r", [128, D])
                bg1 = [sb(sc1, "bg1_%d" % j, [128, D]) for j in range(2)]
                wr = sb(sc1, "wr", [128, 8, NE])
                xa = [sb(sc1, "xa%d" % i, [128, D]) for i in range(NC3)]
                t1 = [sb(sc1, "t1_%d" % i, [128, D]) for i in range(NC3)]
                xm = [sb(sc1, "xm%d" % i, [128, D]) for i in range(NC3)]
                hmf = [sb(sc1, "hmf%d" % i, [128, D]) for i in range(NC3)]
                xms = [sb(sc1, "xms%d" % i, [128, D]) for i in range(NC3)]
                hmb = [sb(sc1, "hmb%d" % i, [128, D], BF16) for i in range(NC3)]
                hmT = [sb(sc1, "hmT%d" % i, [128, D]) for i in range(NC3)]
                st6c = [[sb(sc1, "st6c%d_%d" % (i, k), [128, 2, 6]) for k in range(2)] for i in range(NC3)]
                mvc = [[sb(sc1, "mvc%d_%d" % (i, k), [128, 2]) for k in range(2)] for i in range(NC3)]
                vec = [[sb(sc1, "vec%d_%d" % (i, k), [128, 1]) for k in range(2)] for i in range(NC3)]
                lnc = [[sb(sc1, "lnc%d_%d" % (i, k), [128, 1]) for k in range(2)] for i in range(NC3)]
                rstdc = [[sb(sc1, "rstdc%d_%d" % (i, k), [128, 1]) for k in range(2)] for i in range(NC3)]
                mx = [sb(sc1, "mx%d" % i, [128, 1]) for i in range(NC3)]
                sm = [sb(sc1, "sm%d" % i, [128, 1]) for i in range(NC3)]
                ex = [sb(sc1, "ex%d" % i, [128, NE]) for i in range(NC3)]
                pC = [pst(sc1, "pC%d" % i, [128, 1024]) for i in range(1)] * NC3
                pT2 = pst(sc1, "pT2", [128, 1024])
                pL = [pst(sc1, "pL%d" % i, [128, 512]) for i in range(NC3)]
                P.dma(G, wout[:], w_out[l].rearrange("(kc p) n -> p kc n", p=128), writes=["wout"])
                P.dma(A_, bout_r[:], prow_d[l][:, PR_BOUT:PR_BOUT + D], writes=["bout_r"])
                P.dma(A_, g1_r[:], prow_d[l][:, PR_G1:PR_G1 + D], writes=["g1_r"])
                P.dma(A_, b1_r[:], prow_d[l][:, PR_B1:PR_B1 + D], writes=["b1_r"])
                P.dma(A_, wr[:], w_router[l].rearrange("(kc p) e -> p kc e", p=128), writes=["wr"])
                for j in range(1 if last else 2):
                    P.op(G, lambda e, j=j: e.tensor_tensor(out=bg1[j][:], in0=bout_r[:], in1=gt1p[j][:], op=ALU.mult), ["bout_r", ("rows", 0, j)], [("bg1", j)])

                def ln_rstd(x_t, b, k, kx):
                    kk = ("lnC", b, k)
                    P.op(V, lambda e: e.bn_stats(out=st6c[b][k][:, 0, :], in_=x_t[:, 0:512]), kx, [kk + ("s",)])
                    P.op(V, lambda e: e.bn_stats(out=st6c[b][k][:, 1, :], in_=x_t[:, 512:1024]), kx, [kk + ("s",)])
                    P.op(V, lambda e: e.bn_aggr(out=mvc[b][k][:], in_=st6c[b][k][:].rearrange("p a b -> p (a b)")), [kk + ("s",)], [kk + ("m",)])
                    P.op(V, lambda e: e.tensor_scalar(out=vec[b][k][:], in0=mvc[b][k][:, 1:2], scalar1=LN_EPS, scalar2=None, op0=ALU.add), [kk + ("m",)], [kk + ("v",)])
                    P.op(A_, lambda e: e.activation(out=lnc[b][k][:], in_=vec[b][k][:], func=AF.Ln), [kk + ("v",)], [kk + ("l",)])
                    P.op(A_, lambda e: e.activation(out=rstdc[b][k][:], in_=lnc[b][k][:], func=AF.Exp, scale=-0.5), [kk + ("l",)], [kk + ("r",)])
                    return [kk + ("m",), kk + ("r",)]

                def c_chain(tcn):
                    j = 0 if tcn < 16 else 1
                    b = tcn % NC3
                    rs_ = slice(tcn * 128, (tcn + 1) * 128)
                    ch = []

                    def c0():
                        P.dma(SY, xa[b][:], xsrc[rs_, :], reads=[("xcur", tcn // 4)], writes=[("xa", b)])
                        P.op(A_, lambda e: e.activation(out=xa[b][:], in_=xa[b][:], func=AF.Identity, bias=0.0, scale=ALPHA), [("xa", b)], [("xa", b)])
                        P.op(G, lambda e: e.tensor_tensor(out=xa[b][:], in0=xa[b][:], in1=bg1[j][:], op=ALU.add), [("xa", b), ("bg1", j)], [("xa", b)])
                    ch.append(c0)

                    def c1():
                        for half in range(2):
                            for kc in range(8):
                                P.op(T_, lambda e, half=half, kc=kc: e.matmul(pC[b][:, half * 512:(half + 1) * 512], lhsT=ycatT[:, kc, rs_],
                                                                            rhs=wout[:, kc, half * 512:(half + 1) * 512], start=(kc == 0), stop=(kc == 7)),
                                     YC_ALL + ["wout"], ["pC"])
                    ch.append(c1)

                    def c2():
                        P.op(V, lambda e: e.tensor_tensor(out=t1[b][:], in0=pC[b][:], in1=gt1p[j][:], op=ALU.mult), ["pC", ("rows", 0, j)], [("t1", b)])
                        P.op(V, lambda e: e.tensor_tensor(out=t1[b][:], in0=t1[b][:], in1=xa[b][:], op=ALU.add), [("t1", b), ("xa", b)], [("t1", b)])
                    ch.append(c2)
                    st = {}

                    def c3():
                        st["k1"] = ln_rstd(t1[b], b, 0, [("t1", b)])
                    ch.append(c3)

                    def c4():
                        P.op(V, lambda e: e.tensor_scalar(out=xm[b][:], in0=t1[b][:], scalar1=mvc[b][0][:, 0:1], scalar2=rstdc[b][0][:], op0=ALU.subtract, op1=ALU.mult),
                             [("t1", b)] + st["k1"], [("xm", b)])
                        P.op(V, lambda e: e.tensor_tensor(out=xm[b][:], in0=xm[b][:], in1=g1_r[:], op=ALU.mult), [("xm", b), "g1_r"], [("xm", b)])
                    ch.append(c4)

                    def c5():
                        P.op(G, lambda e: e.tensor_tensor(out=xm[b][:], in0=xm[b][:], in1=b1_r[:], op=ALU.add), [("xm", b), "b1_r"], [("xm", b)])
                        P.op(A_, lambda e: e.activation(out=xms[b][:], in_=xm[b][:], func=AF.Identity, bias=0.0, scale=ALPHA), [("xm", b)], [("xms", b)])
                        P.dma(SY, acc_d[rs_, :], xms[b][:], reads=[("xms", b)], writes=[("acc_z", tcn)])
                    ch.append(c5)

                    def c6():
                        st["k2"] = ln_rstd(xm[b], b, 1, [("xm", b)])
                    ch.append(c6)

                    def c7():
                        P.op(V, lambda e: e.tensor_scalar(out=hmf[b][:], in0=xm[b][:], scalar1=mvc[b][1][:, 0:1], scalar2=rstdc[b][1][:], op0=ALU.subtract, op1=ALU.mult),
                             [("xm", b)] + st["k2"], [("hmf", b)])
                        P.op(V, lambda e: e.tensor_tensor(out=hmf[b][:], in0=hmf[b][:], in1=sc2p[j][:], op=ALU.mult), [("hmf", b), ("rows", 2, j)], [("hmf", b)])
                    ch.append(c7)

                    def c8():
                        P.op(G, lambda e: e.tensor_tensor(out=hmf[b][:], in0=hmf[b][:], in1=sh2r[j][:], op=ALU.add), [("hmf", b), ("rows", 1, j)], [("hmf", b)])
                    ch.append(c8)

                    def c9():
                        P.op(A_, lambda e: e.activation(out=hmb[b][:], in_=hmf[b][:], func=AF.Identity, bias=0.0, scale=1.0), [("hmf", b)], [("hmb", b)])
                        P.dma(SY, hm_d[rs_, :], hmb[b][:], reads=[("hmb", b)], writes=[("hm_d", tcn)])
                        for kc in range(8):
                            P.op(T_, lambda e, kc=kc: e.transpose(out=pT2[:, kc * 128:(kc + 1) * 128], in_=hmf[b][:, kc * 128:(kc + 1) * 128], identity=ident_f[:]),
                                 [("hmf", b), "ident_f"], ["pT2"])
                    ch.append(c9)

                    def c10():
                        P.op(A_, lambda e: e.activation(out=hmT[b][:, 0:512], in_=pT2[:, 0:512], func=AF.Identity, bias=0.0, scale=1.0), ["pT2"], [("hmT", b)])
                        P.op(V, lambda e: e.tensor_copy(out=hmT[b][:, 512:1024], in_=pT2[:, 512:1024]), ["pT2"], [("hmT", b)])
                    ch.append(c10)

                    def c11():
                        for kc in range(8):
                            P.op(T_, lambda e, kc=kc: e.matmul(pL[b][:, 0:NE], lhsT=hmT[b][:, kc * 128:(kc + 1) * 128], rhs=wr[:, kc, :], start=(kc == 0), stop=(kc == 7)),
                                 [("hmT", b), "wr"], [("pL", b)])
                    ch.append(c11)

                    def c12():
                        P.op(V, lambda e: e.tensor_reduce(out=mx[b][:], in_=pL[b][:, 0:NE], axis=AX.X, op=ALU.max, negate=True), [("pL", b)], [("mx", b)])
                        P.op(A_, lambda e: e.activation(out=ex[b][:], in_=pL[b][:, 0:NE], func=AF.Exp, bias=mx[b][:], scale=1.0, accum_out=sm[b][:]),
                             [("pL", b), ("mx", b)], [("ex", b), ("sm", b)], multi=True)
                    ch.append(c12)

                    def c13():
                        P.op(V, lambda e: e.reciprocal(out=sm[b][:], in_=sm[b][:]), [("sm", b)], [("sm", b)])
                        P.op(V, lambda e: e.tensor_scalar(out=aff_all[:, tcn, :], in0=ex[b][:], scalar1=sm[b][:], scalar2=None, op0=ALU.mult), [("ex", b), ("sm", b)], [("aff_all", tcn)])
                        P.op(T_, lambda e: e.transpose(out=pL[b][0:NE, 128:256], in_=aff_all[:, tcn, :], identity=ident_f[:]), [("aff_all", tcn), "ident_f"], [("pL", b)])
                    ch.append(c13)

                    def c14():
                        P.op(V, lambda e: e.tensor_copy(out=affT[:, rs_], in_=pL[b][0:NE, 128:256]), [("pL", b)], ["affT"])
                    ch.append(c14)
                    return ch

                interleave([c_chain(t_) for t_ in range(nch_out)], NC3, stagger=5)
              P.barrier()
              dump("affT%d" % l, affT[:], [16, NT], F32, "affT")
              check_stop("C_%d" % l)

            sy.close()
            with contextlib.ExitStack() as sd:
              maskT = sb(sd, "maskT", [16, NT])
              m8 = sb(sd, "m8", [16, 8])
              m8c = sb(sd, "m8c", [16, 8])
              Mf = sb(sd, "Mf", [128, NCH, NE])
              Mb = sb(sd, "Mb", [128, NCH, NE], BF16)
              pos = sb(sd, "pos", [128, NCH, NE])
              tvall = sb(sd, "tvall", [128, NCH, NE, 5], BF16)
              Pe = [sb(sd, "Pe%d" % i, [128, NCH, 256], BF16) for i in range(1)]
              idxf = sb(sd, "idxf", [128, 4])
              idxi = [sb(sd, "idxi%d" % i, [128, 4], I32) for i in range(3)]
              gsl = [sb(sd, "gsl%d" % i, [128, 4]) for i in range(3)]
              nsl = 2 if last else 3
              NTOK = 256 if last else 288
              xs = [[sb(sd, "xs%d_%d" % (i, s_), [128, D], BF16) for s_ in range(nsl)] for i in range(2)]
              xsT = [sb(sd, "xsT%d" % i, [128, 8, 288], BF16) for i in range(2)]
              NWT = 7
              Wt = [sb(sd, "Wt%d" % i, [128, 8, D], BF16) for i in range(NWT)]
              actT = [sb(sd, "actT%d" % i, [128, 8, 288], BF16) for i in range(2)]
              sgtd = [sb(sd, "sgtd%d" % i, [128, 288]) for i in range(2)]
              ye = [[sb(sd, "ye%d_%d" % (i, s_), [128, D]) for s_ in range(nsl)] for i in range(1)]
              pI = pst(sd, "pI", [128, 512])
              pX = pst(sd, "pX", [128, 1024], BF16)
              pGt = [pst(sd, "pGt%d" % i, [128, 512]) for i in range(2)]
              pU = [pst(sd, "pU%d" % i, [128, 512]) for i in range(2)]
              pY = pst(sd, "pY", [128, 1024])
              nchm = nch_out
              for i_ in range(1):
                  for s_ in range(nsl):
                      P.op(V, lambda e, s_=s_, i_=i_: e.memset(ye[i_][s_][:], 0.0), [], [("ye", i_, s_)])
              P.op(V, lambda e: e.tensor_copy(out=maskT[:, 0:S], in_=affT[:, 0:S]), ["affT"], ["maskT"])
              for r in range(CAP // 8):
                  P.op(V, lambda e: e.max(out=m8[:], in_=maskT[:, 0:S]), ["maskT"], ["m8"])
                  if r < CAP // 8 - 1:
                      P.op(V, lambda e: e.match_replace(out=maskT[:, 0:S], in_to_replace=m8[:], in_values=maskT[:, 0:S], imm_value=-1.0), ["maskT", "m8"], ["maskT"])
              P.op(V, lambda e: e.tensor_scalar(out=maskT[:, 0:S], in0=affT[:, 0:S], scalar1=m8[:, 7:8], scalar2=None, op0=ALU.is_ge), ["affT", "m8"], ["maskT"])
              if not last:
                  P.op(V, lambda e: e.tensor_copy(out=maskT[:, S:NT], in_=affT[:, S:NT]), ["affT"], ["maskT"])
                  for r in range(CAPC // 8):
                      P.op(V, lambda e: e.max(out=m8c[:], in_=maskT[:, S:NT]), ["maskT"], ["m8c"])
                      if r < CAPC // 8 - 1:
                          P.op(V, lambda e: e.match_replace(out=maskT[:, S:NT], in_to_replace=m8c[:], in_values=maskT[:, S:NT], imm_value=-1.0), ["maskT", "m8c"], ["maskT"])
                  P.op(V, lambda e: e.tensor_scalar(out=maskT[:, S:NT], in0=affT[:, S:NT], scalar1=m8c[:, 7:8], scalar2=None, op0=ALU.is_ge), ["affT", "m8c"], ["maskT"])
              for tcn in range(nchm):
                  P.op(T_, lambda e, tcn=tcn: e.transpose(out=pI[:, 0:NE], in_=maskT[:, tcn * 128:(tcn + 1) * 128], identity=ident_f[0:NE, 0:NE]), ["maskT", "ident_f"], ["pI"])
                  P.op(V, lambda e, tcn=tcn: e.tensor_copy(out=Mf[:, tcn, :], in_=pI[:, 0:NE]), ["pI"], ["Mf"])
                  P.op(A_, lambda e, tcn=tcn: e.activation(out=Mb[:, tcn, :], in_=pI[:, 0:NE], func=AF.Identity, bias=0.0, scale=1.0), ["pI"], ["Mb"])
              for seg in ([range(16)] if last else [range(16), range(16, 18)]):
                  for c in seg:
                      prev = [c2 for c2 in seg if c2 < c]
                      for c2 in prev:
                          P.op(T_, lambda e, c2=c2, first=(c2 == prev[0]): e.matmul(pI[:, 0:NE], lhsT=ones_b[:], rhs=Mb[:, c2, :], start=first, stop=False), ["ones_b", "Mb"], ["pI"])
                      P.op(T_, lambda e, c=c, first=(len(prev) == 0): e.matmul(pI[:, 0:NE], lhsT=tri_b[:], rhs=Mb[:, c, :], start=first, stop=True), ["tri_b", "Mb"], ["pI"])
                      P.op(V, lambda e, c=c: e.tensor_copy(out=pos[:, c, :], in_=pI[:, 0:NE]), ["pI"], ["pos"])
              for e_ in range(NE):
                  P.op(V, lambda e, e_=e_: e.tensor_copy(out=tvall[:, :, e_, 0:2], in_=tvcp_f[:]), ["tvcp_f"], ["tvall"])
              AA = [("aff_all", t) for t in range(nchm)]
              na = nchm
              P.op(V, lambda e: e.tensor_copy(out=tvall[:, 0:na, :, 2], in_=aff_all[:, 0:na, :]), AA, ["tvall"])
              P.op(V, lambda e: e.tensor_tensor(out=aff_all[:, 0:na, :], in0=aff_all[:, 0:na, :], in1=tvall[:, 0:na, :, 2], op=ALU.subtract), AA + ["tvall"], AA)
              P.op(V, lambda e: e.tensor_copy(out=tvall[:, 0:na, :, 3], in_=aff_all[:, 0:na, :]), AA, ["tvall"])
              P.op(V, lambda e: e.tensor_tensor(out=aff_all[:, 0:na, :], in0=aff_all[:, 0:na, :], in1=tvall[:, 0:na, :, 3], op=ALU.subtract), AA + ["tvall"], AA)
              P.op(V, lambda e: e.tensor_copy(out=tvall[:, 0:na, :, 4], in_=aff_all[:, 0:na, :]), AA, ["tvall"])

              wsrc = [w_gate, w_up, w_down]

              def load_expert_w(e_, which):
                  s_ = 3 * e_ + which
                  P.dma(G, Wt[s_ % NWT][:], wsrc[which][l][e_].rearrange("(kc p) n -> p kc n", p=128), writes=[("Wt", s_ % NWT)])

              def route(e_):
                  eb = e_ % 2
                  for c in range(16):
                      P.op(V, lambda e, c=c: e.tensor_scalar(out=Pe[0][:, c, :], in0=iota_f[:], scalar1=pos[:, c, e_:e_ + 1], scalar2=Mf[:, c, e_:e_ + 1],
                                                             op0=ALU.is_equal, op1=ALU.mult), ["iota_f", "pos", "Mf"], [("Pe", 0)])
                  if not last:
                      for c in (16, 17):
                          P.op(V, lambda e, c=c: e.tensor_scalar(out=Pe[0][:, c, 0:128], in0=iota_f[:, 0:128], scalar1=pos[:, c, e_:e_ + 1], scalar2=Mf[:, c, e_:e_ + 1],
                                                                 op0=ALU.is_equal, op1=ALU.mult), ["iota_f", "pos", "Mf"], [("Pe", 0)])
                  for s_ in range(nsl):
                      cs = range(16) if s_ < 2 else (16, 17)
                      col0 = (s_ % 2) * 128 if s_ < 2 else 0
                      for i, c in enumerate(cs):
                          P.op(T_, lambda e, s_=s_, c=c, i=i, n=len(cs), col0=col0: e.matmul(pI[:, 8 * s_:8 * s_ + 5], lhsT=Pe[0][:, c, col0:col0 + 128], rhs=tvall[:, c, e_, :],
                                                                                         start=(i == 0), stop=(i == n - 1)), [("Pe", 0), "tvall"], ["pI"])
                  for s_ in range(nsl):
                      if s_ < 2:
                          P.op(V, lambda e, s_=s_: e.tensor_scalar(out=idxf[:, s_:s_ + 1], in0=pI[:, 8 * s_:8 * s_ + 1], scalar1=128.0, scalar2=pI[:, 8 * s_ + 1:8 * s_ + 2],
                                                                   op0=ALU.mult, op1=ALU.add), ["pI"], ["idxf"])
                      else:
                          P.op(V, lambda e, s_=s_: e.tensor_scalar(out=idxf[:, s_:s_ + 1], in0=pI[:, 8 * s_:8 * s_ + 1], scalar1=128.0, scalar2=pI[:, 8 * s_ + 1:8 * s_ + 2],
                                                                   op0=ALU.mult, op1=ALU.add), ["pI"], ["idxf"])
                          P.op(V, lambda e, s_=s_: e.tensor_tensor(out=idxf[:, s_:s_ + 1], in0=idxf[:, s_:s_ + 1], in1=dumoff[:], op=ALU.add), ["idxf", "dumoff"], ["idxf"])
                      P.op(V, lambda e, s_=s_: e.tensor_reduce(out=gsl[e_ % 3][:, s_:s_ + 1], in_=pI[:, 8 * s_ + 2:8 * s_ + 5], axis=AX.X, op=ALU.add), ["pI"], [("gsl", e_ % 3)])
                  P.op(V, lambda e: e.tensor_copy(out=idxi[e_ % 3][:, 0:nsl], in_=idxf[:, 0:nsl]), ["idxf"], [("idxi", e_ % 3)])
                  for s_ in range(nsl):
                      P.op(G, lambda e, s_=s_: e.indirect_dma_start(out=xs[eb][s_][:], out_offset=None, in_=hm_d,
                                                                    in_offset=bass.IndirectOffsetOnAxis(ap=idxi[e_ % 3][:, s_:s_ + 1], axis=0)),
                           [("hm_d", t_) for t_ in range(nchm)] + ["hm_pad", ("idxi", e_ % 3)], [("xs", eb, s_)], dma=True)

              def compute(e_):
                  eb = e_ % 2
                  wg, wu, wd = Wt[(3 * e_) % NWT], Wt[(3 * e_ + 1) % NWT], Wt[(3 * e_ + 2) % NWT]
                  kg, ku, kd = ("Wt", (3 * e_) % NWT), ("Wt", (3 * e_ + 1) % NWT), ("Wt", (3 * e_ + 2) % NWT)
                  for s_ in range(nsl):
                      w_ = 128 if s_ < 2 else 32
                      for kc in range(8):
                          P.op(T_, lambda e, s_=s_, kc=kc: e.transpose(out=pX[:, kc * 128:(kc + 1) * 128], in_=xs[eb][s_][:, kc * 128:(kc + 1) * 128], identity=ident_b[:]),
                               [("xs", eb, s_), "ident_b"], ["pX"])
                      P.op(A_, lambda e, s_=s_, w_=w_: e.activation(out=xsT[eb][:, :, s_ * 128:s_ * 128 + w_], in_=pX[:].rearrange("p (k t) -> p k t", k=8)[:, :, 0:w_],
                                                                  func=AF.Identity, bias=0.0, scale=1.0), ["pX"], [("xsT", eb)])
                  for fc in range(8):
                      fb = fc % 2
                      for kc in range(8):
                          P.op(T_, lambda e, fc=fc, kc=kc, fb=fb: e.matmul(pGt[fb][:, 0:NTOK], lhsT=wg[:, kc, fc * 128:(fc + 1) * 128], rhs=xsT[eb][:, kc, 0:NTOK],
                                                                         start=(kc == 0), stop=(kc == 7)), [kg, ("xsT", eb)], [("pGt", fb)])
                      for kc in range(8):
                          P.op(T_, lambda e, fc=fc, kc=kc, fb=fb: e.matmul(pU[fb][:, 0:NTOK], lhsT=wu[:, kc, fc * 128:(fc + 1) * 128], rhs=xsT[eb][:, kc, 0:NTOK],
                                                                         start=(kc == 0), stop=(kc == 7)), [ku, ("xsT", eb)], [("pU", fb)])
                      P.op(A_, lambda e, fb=fb: e.activation(out=sgtd[fb][:, 0:NTOK], in_=pGt[fb][:, 0:NTOK], func=AF.Silu), [("pGt", fb)], [("sgtd", fb)])
                      P.op(V, lambda e, fb=fb, fc=fc: e.tensor_tensor(out=actT[eb][:, fc, 0:NTOK], in0=sgtd[fb][:, 0:NTOK], in1=pU[fb][:, 0:NTOK], op=ALU.mult),
                           [("sgtd", fb), ("pU", fb)], [("actT", eb)])

              def compute_down(e_):
                  eb = e_ % 2
                  wd = Wt[(3 * e_ + 2) % NWT]
                  kd = ("Wt", (3 * e_ + 2) % NWT)
                  prevk = [("acc_z", c2) for c2 in range(nchm)] if e_ == 0 else [("acc_e", e_ - 1, s2) for s2 in range(nsl)]
                  for s_ in range(nsl):
                      rws = 128 if s_ < 2 else 32
                      for nh in range(2):
                          for fc in range(8):
                              P.op(T_, lambda e, s_=s_, nh=nh, fc=fc, rws=rws: e.matmul(pY[0:rws, nh * 512:(nh + 1) * 512], lhsT=actT[eb][:, fc, s_ * 128:s_ * 128 + rws],
                                                                                      rhs=wd[:, fc, nh * 512:(nh + 1) * 512], start=(fc == 0), stop=(fc == 7)),
                                   [kd, ("actT", eb)], [("pY", nh)])
                          P.op(A_, lambda e, s_=s_, rws=rws, nh=nh: e.activation(out=ye[0][s_][0:rws, nh * 512:(nh + 1) * 512], in_=pY[0:rws, nh * 512:(nh + 1) * 512], func=AF.Identity,
                                                                               bias=0.0, scale=gsl[e_ % 3][0:rws, s_:s_ + 1]),
                               [("pY", nh), ("gsl", e_ % 3)], [("ye", 0, s_)])
                      jg = 0 if s_ < 2 else 1
                      P.op(G, lambda e, s_=s_, rws=rws, jg=jg: e.tensor_tensor(out=ye[0][s_][0:rws, :], in0=ye[0][s_][0:rws, :], in1=gt2p[jg][0:rws, :], op=ALU.mult),
                           [("ye", 0, s_), ("rows", 3, jg)], [("ye", 0, s_)])
                  for s_ in range(nsl):
                      P.op(G, lambda e, s_=s_: e.indirect_dma_start(out=acc_d, out_offset=bass.IndirectOffsetOnAxis(ap=idxi[e_ % 3][:, s_:s_ + 1], axis=0),
                                                                    in_=ye[0][s_][:], in_offset=None, compute_op=ALU.add),
                           [("ye", 0, s_), ("idxi", e_ % 3), "acc_pad"] + prevk, [("acc_e", e_, s_)], dma=True)

              for wh in range(3):
                  load_expert_w(0, wh)
              route(0)
              for e_ in range(NE):
                  if e_ + 1 < NE:
                      route(e_ + 1)
                      load_expert_w(e_ + 1, 0)
                      load_expert_w(e_ + 1, 1)
                  compute(e_)
                  if e_ + 1 < NE:
                      load_expert_w(e_ + 1, 2)
                  compute_down(e_)
            P.barrier()
            check_stop("D_%d" % l)

            with contextlib.ExitStack() as se:
                g2_r = sb(se, "g2_r", [128, D])
                b2_r = sb(se, "b2_r", [128, D])
                NB = 4
                ac = [sb(se, "ac%d" % i, [128, D]) for i in range(NB)]
                xo = [sb(se, "xo%d" % i, [128, D]) for i in range(NB)]
                st6e = [sb(se, "st6e%d" % i, [128, 2, 6]) for i in range(NB)]
                mve = [sb(se, "mve%d" % i, [128, 2]) for i in range(NB)]
                vee = [sb(se, "vee%d" % i, [128, 1]) for i in range(NB)]
                lne = [sb(se, "lne%d" % i, [128, 1]) for i in range(NB)]
                rstde = [sb(se, "rstde%d" % i, [128, 1]) for i in range(NB)]
                P.dma(A_, g2_r[:], prow_d[l][:, PR_G2:PR_G2 + D], writes=["g2_r"])
                P.dma(A_, b2_r[:], prow_d[l][:, PR_B2:PR_B2 + D], writes=["b2_r"])

                def e_chain(tcn):
                    j = 0 if tcn < 16 else 1
                    b = tcn % NB
                    rs_ = slice(tcn * 128, (tcn + 1) * 128)
                    ch = []

                    def e0():
                        P.dma(SY, ac[b][:], acc_d[rs_, :], reads=[("acc_e", NE - 1, s2) for s2 in range(2 if last else 3)] + [("acc_z", tcn)], writes=[("ac", b)])
                    ch.append(e0)

                    def e1():
                        P.op(V, lambda e: e.bn_stats(out=st6e[b][:, 0, :], in_=ac[b][:, 0:512]), [("ac", b)], [("st6e", b)])
                        P.op(V, lambda e: e.bn_stats(out=st6e[b][:, 1, :], in_=ac[b][:, 512:1024]), [("ac", b)], [("st6e", b)])
                        P.op(V, lambda e: e.bn_aggr(out=mve[b][:], in_=st6e[b][:].rearrange("p a b -> p (a b)")), [("st6e", b)], [("mve", b)])
                        P.op(V, lambda e: e.tensor_scalar(out=vee[b][:], in0=mve[b][:, 1:2], scalar1=LN_EPS, scalar2=None, op0=ALU.add), [("mve", b)], [("vee", b)])
                    ch.append(e1)

                    def e2():
                        P.op(A_, lambda e: e.activation(out=lne[b][:], in_=vee[b][:], func=AF.Ln), [("vee", b)], [("lne", b)])
                        P.op(A_, lambda e: e.activation(out=rstde[b][:], in_=lne[b][:], func=AF.Exp, scale=-0.5), [("lne", b)], [("rstde", b)])
                    ch.append(e2)

                    def e3():
                        P.op(V, lambda e: e.tensor_scalar(out=xo[b][:], in0=ac[b][:], scalar1=mve[b][:, 0:1], scalar2=rstde[b][:], op0=ALU.subtract, op1=ALU.mult),
                             [("ac", b), ("mve", b), ("rstde", b)], [("xo", b)])
                        P.op(V, lambda e: e.tensor_tensor(out=xo[b][:], in0=xo[b][:], in1=g2_r[:], op=ALU.mult), [("xo", b), "g2_r"], [("xo", b)])
                    ch.append(e3)

                    def e4():
                        P.op(G, lambda e: e.tensor_tensor(out=xo[b][:], in0=xo[b][:], in1=b2_r[:], op=ALU.add), [("xo", b), "b2_r"], [("xo", b)])
                        if last:
                            P.dma(SY, out_d[rs_, :], xo[b][:], reads=[("xo", b)], writes=[("outd", tcn)])
                        else:
                            P.dma(SY, xcur_d[rs_, :], xo[b][:], reads=[("xo", b)], writes=[("xcur", tcn // 4)])
                    ch.append(e4)
                    return ch

                interleave([e_chain(t_) for t_ in range(nch_out)], 4, stagger=1)
            P.barrier()
            if not last:
                dump("xcur%d" % l, xcur_d, [NT, D], F32, [("xcur", t_) for t_ in range(5)])
            check_stop("E_%d" % l)

        for l_ in range(DEPTH):
            layer(l_)

    try:
        body()
    except Stop:
        pass
    P.emit()
    return nc, dump_out


def _host_prep(inputs):
    f = np.float32
    g = {k: np.ascontiguousarray(np.asarray(v, dtype=f)) for k, v in inputs.items()}
    shared = {}
    for k in ["w_mod", "w_in", "w_out", "w_router", "w_gate", "w_up", "w_down"]:
        shared[k] = g[k]
    pcol = np.zeros((128, DEPTH, NPC), f)
    prow = np.zeros((DEPTH, 128, NPR), f)
    for l in range(DEPTH):
        pcol[:, l, PC_BMOD:PC_BMOD + 16] = g["b_mod"][l, :2048].reshape(16, 128).T
        pcol[:, l, PC_BIN:PC_BIN + 18] = g["b_in"][l, :2304].reshape(18, 128).T
        for j in range(2):
            pcol[:, l, PC_WSH + j * 3:PC_WSH + j * 3 + 3] = g["w_short"][l][:, j * 128:(j + 1) * 128].T
            pcol[:, l, PC_WCF + j * 31:PC_WCF + j * 31 + 31] = g["w_conf_dw"][l][:, j * 128:(j + 1) * 128].T
            pcol[:, l, PC_BCF + j] = g["b_conf_dw"][l, j * 128:(j + 1) * 128]
            pcol[:, l, PC_GLN + j] = g["g_conf_ln"][l, j * 128:(j + 1) * 128]
            pcol[:, l, PC_BLN + j] = g["b_conf_ln"][l, j * 128:(j + 1) * 128]
        row = np.concatenate([g["b_mod"][l, 2048:], g["b_in"][l, 2304:], g["b_out"][l], g["g_post1"][l], g["b_post1"][l], g["g_post2"][l], g["b_post2"][l]])
        prow[l] = np.broadcast_to(row[None, :], (128, NPR))
    kc = np.arange(64)[:, None]
    qc = np.arange(64)[None, :]
    dcol = np.clip(kc - qc + 15, 0, 30)
    rpbt = np.zeros((DEPTH, 128, 8, 896), f)
    for l in range(DEPTH):
        for j in range(14):
            d = 6 - j
            rpbt[l, 0:64, :, j * 64:(j + 1) * 64] = np.transpose(g["na_rpb"][l][:, d + 7][:, dcol], (1, 0, 2))
            rpbt[l, 64:128, :, j * 64:(j + 1) * 64] = np.transpose(g["na_rpb"][l][:, d + 8][:, dcol], (1, 0, 2))
    cstart = np.clip(np.arange(64) - 8, 0, 48)
    col_in = (kc >= cstart[None, :]) & (kc < cstart[None, :] + 16)
    cm1 = np.where(col_in, 0.0, -1e30).astype(f)
    cmask = np.tile(np.concatenate([cm1, cm1], 0), (1, 14))
    shared.update(pcol=pcol, prow=prow, rpbt=rpbt, cmask=np.ascontiguousarray(cmask),
                  ident=np.eye(128, dtype=f), iota=np.ascontiguousarray(np.broadcast_to(np.arange(256, dtype=f)[None], (128, 256))),
                  tri=np.triu(np.ones((128, 128), f), 1))
    tvcp = np.zeros((128, NCH, 2), f)
    tvcp[:, :, 0] = np.arange(NCH)[None, :]
    tvcp[:, :, 1] = np.arange(128)[:, None]
    dumoff = np.zeros((128, 1), f)
    dumoff[32:, 0] = NT + np.arange(96)
    shared.update(tvcp=tvcp, dumoff=dumoff)
    in_maps = []
    for b in range(8):
        m = dict(shared)
        m["xin"] = np.ascontiguousarray(np.concatenate([g["x"][b], g["ctx"][b]], 0))
        cc = np.stack([g["c"][b], g["c_ctx"]], 0)
        m["ccT"] = np.ascontiguousarray(cc.reshape(2, 8, 128).transpose(2, 1, 0))
        in_maps.append(m)
    return in_maps


_NC_CACHE = {}


def kernel(**inputs):
    in_maps = _host_prep(inputs)
    if "nc" not in _NC_CACHE:
        _NC_CACHE["nc"] = build()[0]
    nc = _NC_CACHE["nc"]
    res = run_bass_kernel_spmd(nc, in_maps, core_ids=list(range(8)))
    return np.stack([np.asarray(r["out"], dtype=np.float32) for r in res.results], 0)
```

```python
import contextlib
import numpy as np
import concourse.bass as bass
import concourse.mybir as mybir
from concourse.bass_utils import run_bass_kernel_spmd

F32 = mybir.dt.float32
BF16 = mybir.dt.bfloat16
I32 = mybir.dt.int32
AF = mybir.ActivationFunctionType
ALU = mybir.AluOpType
AX = mybir.AxisListType

ENGS = ["sync", "scalar", "gpsimd", "vector", "tensor"]
DMA_POOL = {"sync": 10, "scalar": 6, "gpsimd": 10}

D = 1024
S = 2048
CTX = 256
NT = S + CTX
NCH = NT // 128
DEPTH = 2
NE = 16
CAP = 256
CAPC = 32
LN_EPS = 1e-5
ALPHA = (2.0 * DEPTH) ** 0.25
NPAD = 96
PC_BMOD, PC_BIN, PC_WSH, PC_WCF, PC_BCF, PC_GLN, PC_BLN, NPC = 0, 16, 34, 40, 102, 104, 106, 108
PR_BMOD, PR_BV, PR_BOUT, PR_G1, PR_B1, PR_G2, PR_B2, NPR = 0, 4096, 4608, 5632, 6656, 7680, 8704, 9728
LOFF, COFF, HPW = 15, 15 + S + 30, 15 + S + 30 + CTX + 15


class _Op:
    __slots__ = ("id", "eng", "fn", "deps", "dma", "needed", "tick", "sem", "semval", "prev_same_sem", "multi")

    def __init__(self, id, eng, fn, dma):
        self.id = id
        self.eng = eng
        self.fn = fn
        self.dma = dma
        self.deps = {}
        self.needed = False
        self.tick = None
        self.sem = None
        self.semval = None
        self.prev_same_sem = None
        self.multi = False


class Prog:
    def __init__(self, nc):
        self.nc = nc
        self.ops = {e: [] for e in ENGS}
        self.n = 0
        self.last_writer = {}
        self.readers = {}
        self.bar = {e: {} for e in ENGS}
        self.since_bar = {}

    def op(self, eng, fn, reads=(), writes=(), dma=False, multi=False):
        o = _Op(self.n, eng, fn, dma)
        o.multi = multi
        self.n += 1
        if self.bar[eng]:
            o.deps.update(self.bar[eng])
            self.bar[eng] = {}
            o.multi = True
        for k in reads:
            w = self.last_writer.get(k)
            if w is not None:
                o.deps[w.id] = w
        for k in writes:
            w = self.last_writer.get(k)
            if w is not None:
                o.deps[w.id] = w
            rd = self.readers.get(k)
            if rd:
                for r in rd[0].values():
                    if r.eng == eng and not dma and eng == "tensor":
                        continue
                    o.deps[r.id] = r
                for r in rd[1]:
                    o.deps[r.id] = r
        for k in reads:
            rd = self.readers.setdefault(k, ({}, []))
            if dma:
                rd[1].append(o)
            else:
                rd[0][eng] = o
        for k in writes:
            self.last_writer[k] = o
            self.readers[k] = ({}, [])
        self.ops[eng].append(o)
        if dma:
            self.since_bar[o.id] = o
        else:
            self.since_bar[("c", eng)] = o
        return o

    def dma(self, eng, out, in_, reads=(), writes=(), **kw):
        return self.op(eng, lambda e: e.dma_start(out=out, in_=in_, **kw), reads, writes, dma=True)

    def barrier(self):
        snap = {o.id: o for o in self.since_bar.values()}
        for e in ENGS:
            self.bar[e].update(snap)
        self.since_bar = {}

    def emit(self):
        nc = self.nc
        for e in ENGS:
            for o in self.ops[e]:
                for d in o.deps.values():
                    if not d.dma:
                        if d.eng == "tensor" and o.eng == "tensor" and not o.dma:
                            continue
                        d.needed = True
        ticks = {}
        for e in ENGS:
            t = 0
            for o in self.ops[e]:
                if not o.dma and o.needed:
                    t += 1
                    o.tick = t
            ticks[e] = t
        stack = contextlib.ExitStack()
        esem = {e: stack.enter_context(nc.semaphore("e_" + e)) for e in ENGS}
        dcount = {}
        for e, npool in DMA_POOL.items():
            pool = [stack.enter_context(nc.semaphore("d_%s_%d" % (e, i))) for i in range(npool)]
            last = [None] * npool
            vals = [0] * npool
            i = 0
            for o in self.ops[e]:
                if o.dma:
                    j = i % npool
                    i += 1
                    vals[j] += 16
                    o.sem = pool[j]
                    o.semval = vals[j]
                    o.prev_same_sem = last[j]
                    last[j] = o
            dcount[e] = (pool, vals)
        block = stack.enter_context(nc.Block())

        def run_engine(ename, eh):
            waited = {}
            waited_seq = {}
            for o in self.ops[ename]:
                need = {}
                for d in o.deps.values():
                    if d.dma:
                        key = ("d", id(d.sem))
                        if need.get(key, (None, 0))[1] < d.semval:
                            need[key] = (d.sem, d.semval)
                    else:
                        if d.eng == "tensor" and ename == "tensor" and not o.dma:
                            continue
                        key = ("e", d.eng)
                        if need.get(key, (None, 0))[1] < d.tick:
                            need[key] = (esem[d.eng], d.tick)
                if o.dma and o.prev_same_sem is not None:
                    p = o.prev_same_sem
                    key = ("d", id(p.sem))
                    if need.get(key, (None, 0))[1] < p.semval:
                        need[key] = (p.sem, p.semval)
                todo = []
                ref = waited_seq if (o.dma or o.multi) else waited
                for key, (sem, val) in need.items():
                    if ref.get(key, 0) >= val:
                        continue
                    todo.append((key, sem, val))
                nstand = len(todo) if (o.multi or o.dma) else max(0, len(todo) - 1)
                for key, sem, val in todo[:nstand]:
                    eh.wait_ge(sem, val)
                    waited_seq[key] = max(waited_seq.get(key, 0), val)
                    waited[key] = max(waited.get(key, 0), val)
                ins = o.fn(eh)
                if nstand < len(todo):
                    key, sem, val = todo[-1]
                    ins._wait_ge(sem, val)
                    waited[key] = max(waited.get(key, 0), val)
                if o.dma:
                    ins.then_inc(o.sem, 16)
                elif o.needed:
                    ins.then_inc(esem[ename], 1)
            if ename == "gpsimd":
                for e2 in ENGS:
                    if ticks[e2] > 0 and e2 != "gpsimd":
                        eh.wait_ge(esem[e2], ticks[e2])
                for e2 in DMA_POOL:
                    pool, vals = dcount[e2]
                    for s, v in zip(pool, vals):
                        if v > 0:
                            eh.wait_ge(s, v)

        @block.sync
        def _(eh):
            run_engine("sync", eh)

        @block.scalar
        def _(eh):
            run_engine("scalar", eh)

        @block.vector
        def _(eh):
            run_engine("vector", eh)

        @block.tensor
        def _(eh):
            run_engine("tensor", eh)

        @block.gpsimd
        def _(eh):
            run_engine("gpsimd", eh)

        stack.close()


def interleave(chains, width, stagger=None):
    if not chains:
        return
    L = max(len(c) for c in chains)
    if stagger is None:
        stagger = max(1, L // width)
    active = []
    nxt = 0
    since = stagger
    while nxt < len(chains) or active:
        if len(active) < width and nxt < len(chains) and since >= stagger:
            active.append(iter(chains[nxt]))
            nxt += 1
            since = 0
        since += 1
        for it in list(active):
            th = next(it, None)
            if th is None:
                active.remove(it)
            else:
                th()


def build(stop_after=None, dumps=()):
    nc = bass.Bass("TRN2", target_bir_lowering=False)

    def din(name, shape, dt=F32):
        return nc.dram_tensor(name, list(shape), dt, kind="ExternalInput").ap()

    xin = din("xin", [NT, D])
    ccT_d = din("ccT", [128, 8, 2])
    w_mod = din("w_mod", [DEPTH, D, 6 * D])
    w_in = din("w_in", [DEPTH, D, 2816])
    w_out = din("w_out", [DEPTH, D, D])
    w_router = din("w_router", [DEPTH, D, NE])
    w_gate = din("w_gate", [DEPTH, NE, D, D])
    w_up = din("w_up", [DEPTH, NE, D, D])
    w_down = din("w_down", [DEPTH, NE, D, D])
    pcol_d = din("pcol", [128, DEPTH, NPC])
    prow_d = din("prow", [DEPTH, 128, NPR])
    rpbt_d = din("rpbt", [DEPTH, 128, 8, 896])
    cmask_d = din("cmask", [128, 896])
    ident_d = din("ident", [128, 128])
    iota_d = din("iota", [128, 256])
    tri_d = din("tri", [128, 128])
    tvcp_d = din("tvcp", [128, NCH, 2])
    dumoff_d = din("dumoff", [128, 1])
    out_d = nc.dram_tensor("out", [S, D], F32, kind="ExternalOutput").ap()
    xmid_d = nc.dram_tensor("xmid_s", [NT, D], F32).ap()
    xcur_d = nc.dram_tensor("xcur_s", [NT, D], F32).ap()
    hm_d = nc.dram_tensor("hm_s", [NT + NPAD, D], BF16).ap()
    acc_d = nc.dram_tensor("acc_s", [NT + NPAD, D], F32).ap()

    P = Prog(nc)
    dump_out = {}

    def dump(name, src_ap, shape, dt, key, dst_view=None):
        if name not in dumps:
            return
        t = nc.dram_tensor("dbg_" + name, list(shape), dt, kind="ExternalOutput").ap()
        dump_out[name] = t
        P.dma("sync", dst_view(t) if dst_view else t, src_ap, reads=key if isinstance(key, list) else [key])

    class Stop(Exception):
        pass

    def check_stop(tag):
        if stop_after == tag:
            raise Stop()

    V, A_, G, T_, SY = "vector", "scalar", "gpsimd", "tensor", "sync"

    def body():
      with contextlib.ExitStack() as g0:
        uid = [0]

        def sb(st, name, shape, dt=F32):
            uid[0] += 1
            return st.enter_context(nc.sbuf_tensor("%s_%d" % (name, uid[0]), list(shape), dt))

        def pst(st, name, shape, dt=F32):
            uid[0] += 1
            return st.enter_context(nc.psum_tensor("%s_%d" % (name, uid[0]), list(shape), dt))

        ident_f = sb(g0, "ident_f", [128, 128])
        ident_b = sb(g0, "ident_b", [128, 128], BF16)
        ones_b = sb(g0, "ones_b", [128, 128], BF16)
        ones_f = sb(g0, "ones_f", [128, 128])
        tri_f = sb(g0, "tri_f", [128, 128])
        tri_b = sb(g0, "tri_b", [128, 128], BF16)
        iota_f = sb(g0, "iota_f", [128, 256])
        tvcp_f = sb(g0, "tvcp_f", [128, NCH, 2])
        dumoff = sb(g0, "dumoff", [128, 1])
        pcol = sb(g0, "pcol", [128, DEPTH, NPC])
        ccT = sb(g0, "ccT", [128, 8, 2])
        scT_f = sb(g0, "scT_f", [128, 8, 2])
        scT_b = sb(g0, "scT_b", [128, 8, 2], BF16)
        screp = [sb(g0, "screp%d" % j, [128, 8, 128], BF16) for j in range(2)]
        neghalf = sb(g0, "neghalf", [128, 32])
        P.dma(SY, ident_f[:], ident_d, writes=["ident_f"])
        P.dma(SY, tri_f[:], tri_d, writes=["tri_f"])
        P.dma(SY, iota_f[:], iota_d, writes=["iota_f"])
        P.dma(SY, tvcp_f[:], tvcp_d, writes=["tvcp_f"])
        P.dma(SY, dumoff[:], dumoff_d, writes=["dumoff"])
        P.dma(SY, pcol[:], pcol_d, writes=["pcol"])
        P.dma(SY, ccT[:], ccT_d, writes=["ccT"])
        P.op(V, lambda e: e.tensor_copy(out=ident_b[:], in_=ident_f[:]), ["ident_f"], ["ident_b"])
        P.op(V, lambda e: e.tensor_copy(out=tri_b[:], in_=tri_f[:]), ["tri_f"], ["tri_b"])
        P.op(V, lambda e: e.memset(ones_b[:], 1.0), [], ["ones_b"])
        P.op(V, lambda e: e.memset(ones_f[:], 1.0), [], ["ones_f"])
        P.op(V, lambda e: e.memset(neghalf[:], -0.5), [], ["neghalf"])
        P.op(A_, lambda e: e.activation(out=scT_f[:], in_=ccT[:], func=AF.Silu), ["ccT"], ["scT_f"])
        P.op(V, lambda e: e.tensor_copy(out=scT_b[:], in_=scT_f[:]), ["scT_f"], ["scT_b"])
        for j in range(2):
            for kc in range(8):
                P.op(V, lambda e, j=j, kc=kc: e.tensor_scalar(out=screp[j][:, kc, :], in0=ones_f[:], scalar1=scT_f[:, kc, j:j + 1],
                                                              scalar2=None, op0=ALU.mult), ["ones_f", "scT_f"], [("screp", j)])
        with contextlib.ExitStack() as sz:
            hz = sb(sz, "hz", [128, D], BF16)
            zero_t = sb(sz, "zero_t", [128, D])
            P.op(V, lambda e: e.memset(zero_t[:], 0.0), [], ["zero_t"])
            P.dma(SY, acc_d[NT:NT + NPAD, :], zero_t[0:NPAD, :], reads=["zero_t"], writes=["acc_pad"])
            P.op(V, lambda e: e.memset(hz[:], 0.0), [], ["hz"])
            P.dma(SY, hm_d[NT:NT + NPAD, :], hz[0:NPAD, :], reads=["hz"], writes=["hm_pad"])
        P.barrier()

        def pc(l, c):
            return pcol[:, l, c:c + 1]

        def ln_stats(e_key, x_ap, st6, mv, keys_r, keys_w):
            P.op(V, lambda e: e.bn_stats(out=st6[:, 0, :], in_=x_ap[:, 0:512]), keys_r, [e_key])
            P.op(V, lambda e: e.bn_stats(out=st6[:, 1, :], in_=x_ap[:, 512:1024]), keys_r, [e_key])
            P.op(V, lambda e: e.bn_aggr(out=mv, in_=st6[:].rearrange("p a b -> p (a b)")), [e_key], keys_w)

        def rstd_nb(mean_ap, var_ap, ve, rstd, nb, n, kr, kw):
            P.op(V, lambda e: e.tensor_scalar(out=ve, in0=var_ap, scalar1=LN_EPS, scalar2=None, op0=ALU.add), kr, [kw + "_ve"])
            P.op(G, lambda e: e.tensor_tensor(out=rstd, in0=ve, in1=neghalf[:, 0:n], op=ALU.pow), [kw + "_ve", "neghalf"], [kw + "_rstd"])
            P.op(V, lambda e: e.scalar_tensor_tensor(out=nb, in0=mean_ap, scalar=-1.0, in1=rstd, op0=ALU.mult, op1=ALU.mult),
                 kr + [kw + "_rstd"], [kw + "_nb"])

        def layer(l):
          last = l == DEPTH - 1
          xsrc = xin if l == 0 else xcur_d
          nch_out = 16 if last else NCH
          with contextlib.ExitStack() as gl:
            gt2p = [sb(gl, "gt2p%d" % j, [128, D]) for j in range(2)]
            modT = sb(gl, "modT", [128, 16, 2])
            affT = sb(gl, "affT", [16, NT])
            aff_all = sb(gl, "aff_all", [128, NCH, NE])
            sy = contextlib.ExitStack()
            ycatT = sb(sy, "ycatT", [128, 8, NT], BF16)

            with contextlib.ExitStack() as s1:
                wm = [sb(s1, "wm%d" % i, [128, 8, 512], BF16) for i in range(4)]
                pm = [pst(s1, "pm%d" % i, [128, 512]) for i in range(2)]
                for g in range(4):
                    P.dma(G, wm[g % 4][:], w_mod[l][:, g * 512:(g + 1) * 512].rearrange("(kc p) n -> p kc n", p=128),
                          writes=[("wm", g % 4)])
                    for c4 in range(4):
                        cc = g * 4 + c4
                        for kc in range(8):
                            P.op(T_, lambda e, g=g, c4=c4, kc=kc, cc=cc: e.matmul(pm[cc % 2][:, 0:2], lhsT=wm[g % 4][:, kc, c4 * 128:(c4 + 1) * 128],
                                                                                 rhs=scT_b[:, kc, :], start=(kc == 0), stop=(kc == 7)),
                                 [("wm", g % 4), "scT_b"], [("pm", cc % 2)])
                        if cc >= 8:
                            P.op(V, lambda e, cc=cc: e.tensor_scalar(out=modT[:, cc, :], in0=pm[cc % 2][:, 0:2], scalar1=pc(l, PC_BMOD + cc),
                                                                     scalar2=1.0, op0=ALU.add, op1=ALU.add), [("pm", cc % 2), "pcol"], ["modT"])
                        else:
                            P.op(V, lambda e, cc=cc: e.tensor_scalar(out=modT[:, cc, :], in0=pm[cc % 2][:, 0:2], scalar1=pc(l, PC_BMOD + cc),
                                                                     scalar2=None, op0=ALU.add), [("pm", cc % 2), "pcol"], ["modT"])
            P.barrier()
            dump("modT%d" % l, modT[:], [128, 16, 2], F32, "modT")
            check_stop("M1_%d" % l)

            with contextlib.ExitStack() as smix:
              hT = sb(smix, "hT", [128, 8, NT], BF16)
              wt = [sb(smix, "wt%d" % i, [128, 8, 512], BF16) for i in range(3)]

              def load_w(i, c0, ncols):
                  P.dma(G, wt[i][:, :, 0:ncols], w_in[l][:, c0:c0 + ncols].rearrange("(kc p) n -> p kc n", p=128), writes=[("wt", i)])

              load_w(0, 0, 512)
              load_w(1, 512, 256)
              load_w(2, 768, 512)
              with contextlib.ExitStack() as sa:
                xg = [sb(sa, "xg%d" % i, [128, 4, D]) for i in range(3)]
                st6a = [sb(sa, "st6a%d" % i, [128, 4, 2, 6]) for i in range(3)]
                mva = [sb(sa, "mva%d" % i, [128, 4, 2]) for i in range(3)]
                vea = [sb(sa, "vea%d" % i, [128, 4]) for i in range(3)]
                lna = [sb(sa, "lna%d" % i, [128, 4]) for i in range(3)]
                rstda = [sb(sa, "rstda%d" % i, [128, 4]) for i in range(3)]
                nba = [sb(sa, "nba%d" % i, [128, 4]) for i in range(3)]
                pa = [pst(sa, "pa%d" % i, [128, 512]) for i in range(4)]
                pcount = [0]

                def a_chain(tg):
                    ncg = 4 if tg < 4 else 2
                    j = 0 if tg < 4 else 1
                    b = tg % 3
                    ch = []

                    def a0():
                        P.dma(SY, xg[b][:, 0:ncg, :], xsrc[tg * 512:tg * 512 + ncg * 128, :].rearrange("(c p) d -> p c d", p=128),
                              reads=[("xcur", tg)], writes=[("xg", b, c) for c in range(ncg)])
                    ch.append(a0)
                    for c in range(ncg):
                        def a1(c=c):
                            P.op(V, lambda e: e.bn_stats(out=st6a[b][:, c, 0, :], in_=xg[b][:, c, 0:512]), [("xg", b, c)], [("st6a", b, c)])
                            P.op(V, lambda e: e.bn_stats(out=st6a[b][:, c, 1, :], in_=xg[b][:, c, 512:1024]), [("xg", b, c)], [("st6a", b, c)])
                            P.op(V, lambda e: e.bn_aggr(out=mva[b][:, c, :], in_=st6a[b][:, c, :, :].rearrange("p a b -> p (a b)")), [("st6a", b, c)], [("mva", b)])
                        ch.append(a1)

                    def a2():
                        P.op(V, lambda e: e.tensor_scalar(out=vea[b][:, 0:ncg], in0=mva[b][:, 0:ncg, 1], scalar1=LN_EPS, scalar2=None, op0=ALU.add), [("mva", b)], [("vea", b)])
                        P.op(A_, lambda e: e.activation(out=lna[b][:, 0:ncg], in_=vea[b][:, 0:ncg], func=AF.Ln), [("vea", b)], [("lna", b)])
                        P.op(A_, lambda e: e.activation(out=rstda[b][:, 0:ncg], in_=lna[b][:, 0:ncg], func=AF.Exp, scale=-0.5), [("lna", b)], [("rstda", b)])
                        P.op(V, lambda e: e.scalar_tensor_tensor(out=nba[b][:, 0:ncg], in0=mva[b][:, 0:ncg, 0], scalar=-1.0, in1=rstda[b][:, 0:ncg], op0=ALU.mult, op1=ALU.mult),
                             [("mva", b), ("rstda", b)], [("nba", b)])
                    ch.append(a2)
                    for c in range(ncg):
                        def a3(c=c):
                            P.op(A_, lambda e: e.activation(out=xg[b][:, c, :], in_=xg[b][:, c, :], func=AF.Identity, bias=nba[b][:, c:c + 1], scale=rstda[b][:, c:c + 1]),
                                 [("xg", b, c), ("rstda", b), ("nba", b)], [("xg", b, c)])
                        ch.append(a3)
                    for kc in range(8):
                        def a4(kc=kc):
                            pb = pcount[0] % 4
                            pcount[0] += 1
                            for c in range(ncg):
                                P.op(T_, lambda e, c=c: e.transpose(out=pa[pb][:, c * 128:(c + 1) * 128], in_=xg[b][:, c, kc * 128:(kc + 1) * 128], identity=ident_f[:]),
                                     [("xg", b, c), "ident_f"], [("pa", pb)])
                            if kc % 2 == 0:
                                P.op(V, lambda e: e.tensor_scalar(out=hT[:, kc, tg * 512:tg * 512 + ncg * 128], in0=pa[pb][:, 0:ncg * 128],
                                                                  scalar1=modT[:, 8 + kc, j:j + 1], scalar2=modT[:, kc, j:j + 1], op0=ALU.mult, op1=ALU.add),
                                     [("pa", pb), "modT"], [("hT", tg)])
                            else:
                                P.op(A_, lambda e: e.activation(out=hT[:, kc, tg * 512:tg * 512 + ncg * 128], in_=pa[pb][:, 0:ncg * 128], func=AF.Identity,
                                                                bias=modT[:, kc, j:j + 1], scale=modT[:, 8 + kc, j:j + 1]), [("pa", pb), "modT"], [("hT", tg)])
                        ch.append(a4)
                    return ch

                interleave([a_chain(tg) for tg in range(5)], 3, stagger=5)
              P.barrier()
              HT_ALL = [("hT", tg) for tg in range(5)]
              dump("hT%d" % l, hT[:], [128, 8, NT], BF16, ("hT", 0))
              check_stop("A_%d" % l)

              def proj_fm(ps_ap, wti, lc, nt, n, pkey):
                  for kc in range(8):
                      P.op(T_, lambda e, kc=kc: e.matmul(ps_ap, lhsT=wt[wti][:, kc, lc * 128:(lc + 1) * 128], rhs=hT[:, kc, nt * 512:nt * 512 + n],
                                                         start=(kc == 0), stop=(kc == 7)), [("wt", wti)] + HT_ALL, [pkey])

              ntl = [(nt, 512) for nt in range(4)] + [(4, 256)]

              with contextlib.ExitStack() as sb1:
                cgs = sb(sb1, "cgs", [128, NT + 4])
                zp = sb(sb1, "zp", [128, NT + 4])
                acc = sb(sb1, "acc", [128, NT + 4])
                pb1 = [pst(sb1, "pb1_%d" % i, [128, 512]) for i in range(4)]
                P.op(G, lambda e: e.memset(zp[:], 0.0), [], ["zp"])
                P.op(G, lambda e: e.memset(acc[:], 0.0), [], ["acc"])

                def pcs(nt):
                    return 1 + nt * 512 if nt < 4 else 2051
                ib = 0
                for j in range(2):
                    for nt, n in ntl:
                        pb = ib % 4; ib += 1
                        proj_fm(pb1[pb][:, 0:n], 0, 2 + j, nt, n, ("pb1", pb))
                        P.op(A_, lambda e, pb=pb, nt=nt, n=n, j=j: e.activation(out=cgs[:, pcs(nt):pcs(nt) + n], in_=pb1[pb][:, 0:n], func=AF.Identity,
                                                                              bias=pc(l, PC_BIN + 2 + j), scale=1.0), [("pb1", pb), "pcol"], ["cgs"])
                        pb = ib % 4; ib += 1
                        proj_fm(pb1[pb][:, 0:n], 1, j, nt, n, ("pb1", pb))
                        P.op(V, lambda e, pb=pb, nt=nt, n=n, j=j: e.scalar_tensor_tensor(out=zp[:, pcs(nt):pcs(nt) + n], in0=pb1[pb][:, 0:n], scalar=pc(l, PC_BIN + 4 + j),
                                                                                       in1=cgs[:, pcs(nt):pcs(nt) + n], op0=ALU.add, op1=ALU.mult),
                             [("pb1", pb), "pcol", "cgs"], ["zp"])
                    W = NT + 2
                    P.op(V, lambda e, j=j: e.tensor_scalar(out=acc[:, 1:1 + W], in0=zp[:, 0:W], scalar1=pc(l, PC_WSH + j * 3 + 0), scalar2=None, op0=ALU.mult),
                         ["zp", "pcol"], ["acc"])
                    P.op(V, lambda e, j=j: e.scalar_tensor_tensor(out=acc[:, 1:1 + W], in0=zp[:, 1:1 + W], scalar=pc(l, PC_WSH + j * 3 + 1), in1=acc[:, 1:1 + W],
                                                                  op0=ALU.mult, op1=ALU.add), ["zp", "pcol", "acc"], ["acc"])
                    P.op(V, lambda e, j=j: e.scalar_tensor_tensor(out=acc[:, 1:1 + W], in0=zp[:, 2:2 + W], scalar=pc(l, PC_WSH + j * 3 + 2), in1=acc[:, 1:1 + W],
                                                                  op0=ALU.mult, op1=ALU.add), ["zp", "pcol", "acc"], ["acc"])
                    for nt, n in ntl:
                        pb = ib % 4; ib += 1
                        proj_fm(pb1[pb][:, 0:n], 0, j, nt, n, ("pb1", pb))
                        P.op(V, lambda e, pb=pb, nt=nt, n=n, j=j: e.scalar_tensor_tensor(out=ycatT[:, j, nt * 512:nt * 512 + n], in0=pb1[pb][:, 0:n], scalar=pc(l, PC_BIN + j),
                                                                                       in1=acc[:, pcs(nt):pcs(nt) + n], op0=ALU.add, op1=ALU.mult),
                             [("pb1", pb), "pcol", "acc"], [("ycatT", j)])
              P.barrier()
              check_stop("B1_%d" % l)

              with contextlib.ExitStack() as sb2:
                hp = [sb(sb2, "hp%d" % j, [128, HPW], BF16) for j in range(2)]
                Dm = sb(sb2, "Dm", [128, 62, 128], BF16)
                cv = [sb(sb2, "cv%d" % j, [128, NT]) for j in range(2)]
                cvT = sb(sb2, "cvT", [128, NCH, 256])
                sgt2 = [sb(sb2, "sgt%d" % i, [128, 512]) for i in range(2)]
                st2 = sb(sb2, "st2", [128, NCH, 6])
                mv2 = sb(sb2, "mv2", [128, NCH, 2])
                ve2 = sb(sb2, "ve2", [128, NCH])
                rstd2 = sb(sb2, "rstd2", [128, NCH])
                nb2 = sb(sb2, "nb2", [128, NCH])
                pb2 = [pst(sb2, "pb2_%d" % i, [128, 512]) for i in range(6)]
                load_w(0, 1280, 512)
                load_w(1, 1792, 512)
                for j in range(2):
                    P.op(G, lambda e, j=j: e.memset(hp[j][:], 0.0), [], [("hp", j)])
                for i in range(62):
                    P.op(V, lambda e, i=i: e.tensor_scalar(out=Dm[:, i, :], in0=ident_f[:], scalar1=pc(l, PC_WCF + i), scalar2=None, op0=ALU.mult),
                         ["ident_f", "pcol"], ["Dm"])

                def hoff(nt):
                    return LOFF + nt * 512 if nt < 4 else COFF
                ib = 0
                for j in range(2):
                    for nt, n in ntl:
                        pb = ib % 6; ib += 1
                        proj_fm(pb2[pb][:, 0:n], 2, 2 + j, nt, n, ("pb2", pb))
                        sgi = ib % 2
                        P.op(A_, lambda e, pb=pb, n=n, j=j, sgi=sgi: e.activation(out=sgt2[sgi][:, 0:n], in_=pb2[pb][:, 0:n], func=AF.Sigmoid,
                                                                                bias=pc(l, PC_BIN + 8 + j), scale=1.0), [("pb2", pb), "pcol"], [("sgt", sgi)])
                        pb = ib % 6; ib += 1
                        proj_fm(pb2[pb][:, 0:n], 2, j, nt, n, ("pb2", pb))
                        P.op(V, lambda e, pb=pb, nt=nt, n=n, j=j, sgi=sgi: e.scalar_tensor_tensor(out=hp[j][:, hoff(nt):hoff(nt) + n], in0=pb2[pb][:, 0:n],
                                                                                                scalar=pc(l, PC_BIN + 6 + j), in1=sgt2[sgi][:, 0:n], op0=ALU.add, op1=ALU.mult),
                             [("pb2", pb), "pcol", ("sgt", sgi)], [("hp", j)])
                for j in range(2):
                    for nt, n in ntl:
                        pb = ib % 6; ib += 1
                        base = hoff(nt) - 15
                        for k in range(31):
                            P.op(T_, lambda e, pb=pb, j=j, k=k, base=base, n=n: e.matmul(pb2[pb][:, 0:n], lhsT=Dm[:, j * 31 + k, :], rhs=hp[j][:, base + k:base + k + n],
                                                                                       start=(k == 0), stop=(k == 30)), ["Dm", ("hp", j)], [("pb2", pb)])
                        P.op(A_, lambda e, pb=pb, j=j, nt=nt, n=n: e.activation(out=cv[j][:, nt * 512:nt * 512 + n], in_=pb2[pb][:, 0:n], func=AF.Identity,
                                                                              bias=pc(l, PC_BCF + j), scale=1.0), [("pb2", pb), "pcol"], [("cv", j)])
                for tcn in range(NCH):
                    pb = ib % 6; ib += 1
                    for j in range(2):
                        P.op(T_, lambda e, pb=pb, j=j, tcn=tcn: e.transpose(out=pb2[pb][:, j * 128:(j + 1) * 128], in_=cv[j][:, tcn * 128:(tcn + 1) * 128], identity=ident_f[:]),
                             [("cv", j), "ident_f"], [("pb2", pb)])
                    P.op(V, lambda e, pb=pb, tcn=tcn: e.tensor_copy(out=cvT[:, tcn, :], in_=pb2[pb][:, 0:256]), [("pb2", pb)], [("cvT", tcn)])
                    P.op(V, lambda e, tcn=tcn: e.bn_stats(out=st2[:, tcn, :], in_=cvT[:, tcn, :]), [("cvT", tcn)], [("st2", tcn)])
                    P.op(V, lambda e, tcn=tcn: e.bn_aggr(out=mv2[:, tcn, :], in_=st2[:, tcn, :]), [("st2", tcn)], ["mv2"])
                rstd_nb(mv2[:, :, 0], mv2[:, :, 1], ve2[:], rstd2[:], nb2[:], NCH, ["mv2"], "B2")
                for tcn in range(NCH):
                    P.op(A_, lambda e, tcn=tcn: e.activation(out=cvT[:, tcn, :], in_=cvT[:, tcn, :], func=AF.Identity, bias=nb2[:, tcn:tcn + 1], scale=rstd2[:, tcn:tcn + 1]),
                         [("cvT", tcn), "B2_rstd", "B2_nb"], [("cvT", tcn)])
                for g4 in range(5):
                    ncg = 4 if g4 < 4 else 2
                    for j in range(2):
                        pb = ib % 6; ib += 1
                        for c in range(ncg):
                            tcn = g4 * 4 + c
                            P.op(T_, lambda e, pb=pb, c=c, j=j, tcn=tcn: e.transpose(out=pb2[pb][:, c * 128:(c + 1) * 128], in_=cvT[:, tcn, j * 128:(j + 1) * 128], identity=ident_f[:]),
                                 [("cvT", tcn), "ident_f"], [("pb2", pb)])
                        P.op(A_, lambda e, pb=pb, j=j, g4=g4, ncg=ncg: e.activation(out=ycatT[:, 2 + j, g4 * 512:g4 * 512 + ncg * 128], in_=pb2[pb][:, 0:ncg * 128], func=AF.Silu,
                                                                                  bias=pc(l, PC_BLN + j), scale=pc(l, PC_GLN + j)), [("pb2", pb), "pcol"], [("ycatT", 2 + j)])
              P.barrier()
              check_stop("B2_%d" % l)

              with contextlib.ExitStack() as sb3:
                vaug = sb(sb3, "vaug", [128, NCH, 8, 65], BF16)
                bv_r = sb(sb3, "bv_r", [128, 512])
                cm = sb(sb3, "cm", [128, 896])
                qT = [sb(sb3, "qT%d" % i, [128, NT], BF16) for i in range(2)]
                kT = [sb(sb3, "kT%d" % i, [128, NT], BF16) for i in range(2)]
                tb = [sb(sb3, "tb%d" % i, [128, 2, 896]) for i in range(2)]
                bq8 = sb(sb3, "bq8", [128, 4])
                sS = [sb(sb3, "sS%d" % i, [128, 640]) for i in range(3)]
                PT = [sb(sb3, "PT%d" % i, [128, 896], BF16) for i in range(3)]
                rsb = [sb(sb3, "rsb%d" % i, [128, 1]) for i in range(3)]
                ychp = sb(sb3, "ychp", [128, NCH, 128])
                NU = 3
                pAb = [pst(sb3, "pAb%d" % i, [128, 512]) for i in range(NU)]
                pBb = [pst(sb3, "pBb%d" % i, [128, 512]) for i in range(NU)]
                pO = [pBb[i][:, 384:449] for i in range(NU)]
                pG = [pst(sb3, "pG%d" % i, [128, 512]) for i in range(2)]
                load_w(2, 2304, 512)
                P.dma(A_, bv_r[:], prow_d[l][:, PR_BV:PR_BV + 512], writes=["bv_r"])
                P.dma(A_, cm[:], cmask_d, writes=["cm"])
                P.op(G, lambda e: e.memset(vaug[:], 1.0), [], ["vaug"])
                P.op(V, lambda e: e.tensor_scalar(out=bq8[:], in0=pcol[:, l, PC_BIN + 10:PC_BIN + 14], scalar1=0.125, scalar2=None, op0=ALU.mult), ["pcol"], ["bq8"])
                ib = 0
                for tcn in range(NCH):
                    pb = ib % 2; ib += 1
                    for kc in range(8):
                        P.op(T_, lambda e, pb=pb, kc=kc, tcn=tcn: e.matmul(pG[pb][:, 0:512], lhsT=hT[:, kc, tcn * 128:(tcn + 1) * 128], rhs=wt[2][:, kc, 0:512],
                                                                         start=(kc == 0), stop=(kc == 7)), [("wt", 2)] + HT_ALL, [("pG", pb)])
                    P.op(V, lambda e, pb=pb, tcn=tcn: e.tensor_tensor(out=vaug[:, tcn, :, 0:64], in0=pG[pb][:, 0:512].rearrange("p (h d) -> p h d", h=8),
                                                                    in1=bv_r[:].rearrange("p (h d) -> p h d", h=8), op=ALU.add), [("pG", pb), "bv_r"], ["vaug"])
                nqt = 16 if last else NCH
                u = 0
                for hpi in range(4):
                    qb = hpi % 2
                    for nt, n in ntl:
                        if last and nt == 4:
                            continue
                        pb = ib % 2; ib += 1
                        proj_fm(pG[pb][:, 0:n], 0, hpi, nt, n, ("pG", pb))
                        P.op(A_, lambda e, pb=pb, nt=nt, n=n, qb=qb, hpi=hpi: e.activation(out=qT[qb][:, nt * 512:nt * 512 + n], in_=pG[pb][:, 0:n], func=AF.Identity,
                                                                                         bias=bq8[:, hpi:hpi + 1], scale=0.125), [("pG", pb), "bq8"], [("qT", qb)])
                    for nt, n in ntl:
                        pb = ib % 2; ib += 1
                        proj_fm(pG[pb][:, 0:n], 1, hpi, nt, n, ("pG", pb))
                        P.op(A_, lambda e, pb=pb, nt=nt, n=n, qb=qb, hpi=hpi: e.activation(out=kT[qb][:, nt * 512:nt * 512 + n], in_=pG[pb][:, 0:n], func=AF.Identity,
                                                                                         bias=pc(l, PC_BIN + 14 + hpi), scale=1.0), [("pG", pb), "pcol"], [("kT", qb)])
                    P.dma(SY, tb[qb][:], rpbt_d[l][:, 2 * hpi:2 * hpi + 2, :], writes=[("tb", qb)])
                    for hh in range(2):
                        P.op(V, lambda e, qb=qb, hh=hh: e.tensor_tensor(out=tb[qb][:, hh, :], in0=tb[qb][:, hh, :], in1=cm[:], op=ALU.add), [("tb", qb), "cm"], [("tb", qb)])
                    def unit_chain(hh, qt, ub, qb=qb, hpi=hpi):
                        h = 2 * hpi + hh
                        r0, r1 = hh * 64, hh * 64 + 64
                        if qt < 16:
                            r = 2 * qt
                            if r <= 2:
                                krs = [6, 4, 2, 0]
                            elif r >= 28:
                                krs = [30, 28, 26, 24]
                            else:
                                krs = [r + 4, r + 2, r, r - 2, r - 4]
                            interior = 4 <= r <= 26
                            s0 = (6 - (krs[0] - r)) * 64
                            btc = [kr // 2 for kr in krs]
                        else:
                            interior = False
                            s0 = 0
                            btc = []
                        nbnd = len(btc)
                        nA = min(nbnd, 4)
                        nBb = nbnd - nA
                        tcsB = btc[nA:] + [16, 17]
                        tcs = btc + [16, 17]
                        nck = len(tcs)
                        kA, kB = ("pAb", ub), ("pBb", ub)
                        ch = []

                        def mmS(dst, i, tck):
                            P.op(T_, lambda e: e.matmul(dst, lhsT=kT[qb][r0:r1, tck * 128:(tck + 1) * 128], rhs=qT[qb][r0:r1, qt * 128:(qt + 1) * 128],
                                                        start=True, stop=True), [("kT", qb), ("qT", qb)], [kA if i < nA else kB])
                        if nA:
                            def s1a():
                                for i in range(nA):
                                    mmS(pAb[ub][:, i * 128:(i + 1) * 128], i, btc[i])
                            ch.append(s1a)

                        def s1b():
                            for i2, tck in enumerate(tcsB):
                                mmS(pBb[ub][:, i2 * 128:(i2 + 1) * 128], nA + i2, tck)
                        ch.append(s1b)
                        if nA:
                            def s2a():
                                P.op(V, lambda e: e.tensor_tensor(out=sS[ub][:, 0:nA * 128], in0=pAb[ub][:, 0:nA * 128],
                                                                  in1=tb[qb][:, hh, s0:s0 + nA * 128], op=ALU.add), [kA, ("tb", qb)], [("sSa", ub)])
                                if interior:
                                    P.op(V, lambda e: e.memset(sS[ub][0:64, 0:64], -1e30), [], [("sSa", ub)])
                            ch.append(s2a)
                        if nBb:
                            def s2b():
                                P.op(V, lambda e: e.tensor_tensor(out=sS[ub][:, 512:640], in0=pBb[ub][:, 0:128],
                                                                  in1=tb[qb][:, hh, s0 + 512:s0 + 640], op=ALU.add), [kB, ("tb", qb)], [("sSb", ub)])
                                P.op(V, lambda e: e.memset(sS[ub][0:64, 512 + 64:640], -1e30), [], [("sSb", ub)])
                            ch.append(s2b)
                        if nA:
                            def s3a():
                                P.op(A_, lambda e: e.activation(out=PT[ub][:, 0:nA * 128], in_=sS[ub][:, 0:nA * 128], func=AF.Exp), [("sSa", ub)], [("PTa", ub)])
                            ch.append(s3a)

                        def s3b():
                            if nBb:
                                P.op(A_, lambda e: e.activation(out=PT[ub][:, 512:640], in_=sS[ub][:, 512:640], func=AF.Exp), [("sSb", ub)], [("PTb", ub)])
                            P.op(A_, lambda e: e.activation(out=PT[ub][:, nbnd * 128:nbnd * 128 + 256], in_=pBb[ub][:, nBb * 128:nBb * 128 + 256], func=AF.Exp),
                                 [kB], [("PTb", ub)])
                        ch.append(s3b)

                        def s4():
                            for i, tck in enumerate(tcs):
                                kr1 = 64 if (interior and i == 0) else 128
                                P.op(T_, lambda e, i=i, tck=tck, kr1=kr1: e.matmul(
                                    pO[ub], lhsT=PT[ub][0:kr1, i * 128:(i + 1) * 128], rhs=vaug[0:kr1, tck, h, :],
                                    start=(i == 0), stop=(i == nck - 1)), [("PTa", ub), ("PTb", ub), "vaug"], [kB])
                        ch.append(s4)

                        def s5():
                            P.op(V, lambda e: e.reciprocal(out=rsb[ub][:], in_=pO[ub][:, 64:65]), [kB], [("rsb", ub)])
                            P.op(V, lambda e: e.tensor_scalar(out=ychp[:, qt, hh * 64:(hh + 1) * 64], in0=pO[ub][:, 0:64], scalar1=rsb[ub][:],
                                                              scalar2=None, op0=ALU.mult), [kB, ("rsb", ub)], [("ychp", qt)])
                        ch.append(s5)
                        return ch

                    chains = []
                    for hh in range(2):
                        for qt in range(nqt):
                            chains.append(unit_chain(hh, qt, u % NU))
                            u += 1
                    interleave(chains, NU, stagger=3)
                    for g4 in range(5):
                        ncg = 4 if g4 < 4 else 2
                        if last and g4 == 4:
                            continue
                        pb = ib % 2; ib += 1
                        for c in range(ncg):
                            qt = g4 * 4 + c
                            P.op(T_, lambda e, pb=pb, c=c, qt=qt: e.transpose(out=pG[pb][:, c * 128:(c + 1) * 128], in_=ychp[:, qt, :], identity=ident_f[:]),
                                 [("ychp", qt), "ident_f"], [("pG", pb)])
                        P.op(A_, lambda e, pb=pb, g4=g4, ncg=ncg, hpi=hpi: e.activation(out=ycatT[:, 4 + hpi, g4 * 512:g4 * 512 + ncg * 128], in_=pG[pb][:, 0:ncg * 128], func=AF.Identity,
                                                                                      bias=0.0, scale=1.0), [("pG", pb)], [("ycatT", 4 + hpi)])
              P.barrier()
            P.barrier()
            YC_ALL = [("ycatT", i) for i in range(8)]
            dump("ycatT%d" % l, ycatT[:], [128, 8, NT], BF16, ("ycatT", 0))
            check_stop("B3_%d" % l)

            with contextlib.ExitStack() as sc:
              gt1p = [sb(sc, "gt1p%d" % j, [128, D]) for j in range(2)]
              sh2r = [sb(sc, "sh2r%d" % j, [128, D]) for j in range(2)]
              sc2p = [sb(sc, "sc2p%d" % j, [128, D]) for j in range(2)]
              rows = {0: gt1p, 1: sh2r, 2: sc2p, 3: gt2p}
              wout = sb(sc, "wout", [128, 8, D], BF16)
              bout_r = sb(sc, "bout_r", [128, D])
              g1_r = sb(sc, "g1_r", [128, D])
              b1_r = sb(sc, "b1_r", [128, D])
              wr = sb(sc, "wr", [128, 8, NE])
              P.dma(G, wout[:], w_out[l].rearrange("(kc p) n -> p kc n", p=128), writes=["wout"])
              P.dma(A_, bout_r[:], prow_d[l][:, PR_BOUT:PR_BOUT + D], writes=["bout_r"])
              P.dma(A_, g1_r[:], prow_d[l][:, PR_G1:PR_G1 + D], writes=["g1_r"])
              P.dma(A_, b1_r[:], prow_d[l][:, PR_B1:PR_B1 + D], writes=["b1_r"])
              P.dma(A_, wr[:], w_router[l].rearrange("(kc p) e -> p kc e", p=128), writes=["wr"])
              with contextlib.ExitStack() as sm2:
                wm2 = [sb(sm2, "wm2_%d" % i, [128, 8, 512], BF16) for i in range(4)]
                bmr = sb(sm2, "bmr", [128, 4096])
                pm2 = [pst(sm2, "pm2_%d" % i, [128, 512]) for i in range(2)]
                P.dma(A_, bmr[:], prow_d[l][:, PR_BMOD:PR_BMOD + 4096], writes=["bmr"])
                ib = 0
                for g in range(8):
                    P.dma(G, wm2[g % 4][:], w_mod[l][:, 2048 + g * 512:2048 + (g + 1) * 512].rearrange("(kc p) n -> p kc n", p=128), writes=[("wm2", g % 4)])
                    which, half = g // 2, g % 2
                    for j in range(2):
                        if last and j == 1:
                            continue
                        pb = ib % 2; ib += 1
                        for kc in range(8):
                            P.op(T_, lambda e, pb=pb, kc=kc, j=j, g=g: e.matmul(pm2[pb][:, 0:512], lhsT=screp[j][:, kc, :], rhs=wm2[g % 4][:, kc, :], start=(kc == 0), stop=(kc == 7)),
                                 [("screp", j), ("wm2", g % 4)], [("pm2", pb)])
                        dst = rows[which][j][:, half * 512:(half + 1) * 512]
                        if which == 1:
                            P.op(V, lambda e, pb=pb, dst=dst, g=g: e.tensor_tensor(out=dst, in0=pm2[pb][:, 0:512], in1=bmr[:, g * 512:(g + 1) * 512], op=ALU.add),
                                 [("pm2", pb), "bmr"], [("rows", which, j)])
                        else:
                            P.op(V, lambda e, pb=pb, dst=dst, g=g: e.scalar_tensor_tensor(out=dst, in0=pm2[pb][:, 0:512], scalar=1.0, in1=bmr[:, g * 512:(g + 1) * 512],
                                                                                       op0=ALU.add, op1=ALU.add), [("pm2", pb), "bmr"], [("rows", which, j)])
              P.barrier()
              check_stop("M2_%d" % l)

              with contextlib.ExitStack() as sc1:
                NC3 = 3
                bg1 = [sb(sc1, "bg1_%d" % j, [128, D]) for j in range(2)]
                xa = [sb(sc1, "xa%d" % i, [128, D]) for i in range(NC3)]
                t1 = [sb(sc1, "t1_%d" % i, [128, D]) for i in range(NC3)]
                xm = [sb(sc1, "xm%d" % i, [128, D]) for i in range(NC3)]
                hmf = [sb(sc1, "hmf%d" % i, [128, D]) for i in range(NC3)]
                xms = [sb(sc1, "xms%d" % i, [128, D]) for i in range(NC3)]
                hmb = [sb(sc1, "hmb%d" % i, [128, D], BF16) for i in range(NC3)]
                hmT = [sb(sc1, "hmT%d" % i, [128, D]) for i in range(NC3)]
                st6c = [[sb(sc1, "st6c%d_%d" % (i, k), [128, 2, 6]) for k in range(2)] for i in range(NC3)]
                mvc = [[sb(sc1, "mvc%d_%d" % (i, k), [128, 2]) for k in range(2)] for i in range(NC3)]
                vec = [[sb(sc1, "vec%d_%d" % (i, k), [128, 1]) for k in range(2)] for i in range(NC3)]
                lnc = [[sb(sc1, "lnc%d_%d" % (i, k), [128, 1]) for k in range(2)] for i in range(NC3)]
                rstdc = [[sb(sc1, "rstdc%d_%d" % (i, k), [128, 1]) for k in range(2)] for i in range(NC3)]
                mx = [sb(sc1, "mx%d" % i, [128, 1]) for i in range(NC3)]
                sm = [sb(sc1, "sm%d" % i, [128, 1]) for i in range(NC3)]
                ex = [sb(sc1, "ex%d" % i, [128, NE]) for i in range(NC3)]
                pC = [pst(sc1, "pC%d" % i, [128, 1024]) for i in range(1)] * NC3
                pT2 = pst(sc1, "pT2", [128, 1024])
                pL = [pst(sc1, "pL%d" % i, [128, 512]) for i in range(NC3)]
                for j in range(1 if last else 2):
                    P.op(G, lambda e, j=j: e.tensor_tensor(out=bg1[j][:], in0=bout_r[:], in1=gt1p[j][:], op=ALU.mult), ["bout_r", ("rows", 0, j)], [("bg1", j)])

                def ln_rstd(x_t, b, k, kx):
                    kk = ("lnC", b, k)
                    P.op(V, lambda e: e.bn_stats(out=st6c[b][k][:, 0, :], in_=x_t[:, 0:512]), kx, [kk + ("s",)])
                    P.op(V, lambda e: e.bn_stats(out=st6c[b][k][:, 1, :], in_=x_t[:, 512:1024]), kx, [kk + ("s",)])
                    P.op(V, lambda e: e.bn_aggr(out=mvc[b][k][:], in_=st6c[b][k][:].rearrange("p a b -> p (a b)")), [kk + ("s",)], [kk + ("m",)])
                    P.op(V, lambda e: e.tensor_scalar(out=vec[b][k][:], in0=mvc[b][k][:, 1:2], scalar1=LN_EPS, scalar2=None, op0=ALU.add), [kk + ("m",)], [kk + ("v",)])
                    P.op(A_, lambda e: e.activation(out=lnc[b][k][:], in_=vec[b][k][:], func=AF.Ln), [kk + ("v",)], [kk + ("l",)])
                    P.op(A_, lambda e: e.activation(out=rstdc[b][k][:], in_=lnc[b][k][:], func=AF.Exp, scale=-0.5), [kk + ("l",)], [kk + ("r",)])
                    return [kk + ("m",), kk + ("r",)]

                def c_chain(tcn):
                    j = 0 if tcn < 16 else 1
                    b = tcn % NC3
                    rs_ = slice(tcn * 128, (tcn + 1) * 128)
                    ch = []

                    def c0():
                        P.dma(A_, xa[b][:], xsrc[rs_, :], reads=[("xcur", tcn // 4)], writes=[("xa", b)])
                        P.op(A_, lambda e: e.activation(out=xa[b][:], in_=xa[b][:], func=AF.Identity, bias=0.0, scale=ALPHA), [("xa", b)], [("xa", b)])
                        P.op(G, lambda e: e.tensor_tensor(out=xa[b][:], in0=xa[b][:], in1=bg1[j][:], op=ALU.add), [("xa", b), ("bg1", j)], [("xa", b)])
                    ch.append(c0)

                    def c1():
                        for half in range(2):
                            for kc in range(8):
                                P.op(T_, lambda e, half=half, kc=kc: e.matmul(pC[b][:, half * 512:(half + 1) * 512], lhsT=ycatT[:, kc, rs_],
                                                                            rhs=wout[:, kc, half * 512:(half + 1) * 512], start=(kc == 0), stop=(kc == 7)),
                                     YC_ALL + ["wout"], ["pC"])
                    ch.append(c1)

                    def c2():
                        P.op(V, lambda e: e.tensor_tensor(out=t1[b][:], in0=pC[b][:], in1=gt1p[j][:], op=ALU.mult), ["pC", ("rows", 0, j)], [("t1", b)])
                        P.op(V, lambda e: e.tensor_tensor(out=t1[b][:], in0=t1[b][:], in1=xa[b][:], op=ALU.add), [("t1", b), ("xa", b)], [("t1", b)])
                    ch.append(c2)
                    st = {}

                    def c3():
                        st["k1"] = ln_rstd(t1[b], b, 0, [("t1", b)])
                    ch.append(c3)

                    def c4():
                        P.op(V, lambda e: e.tensor_scalar(out=xm[b][:], in0=t1[b][:], scalar1=mvc[b][0][:, 0:1], scalar2=rstdc[b][0][:], op0=ALU.subtract, op1=ALU.mult),
                             [("t1", b)] + st["k1"], [("xm", b)])
                        P.op(V, lambda e: e.tensor_tensor(out=xm[b][:], in0=xm[b][:], in1=g1_r[:], op=ALU.mult), [("xm", b), "g1_r"], [("xm", b)])
                    ch.append(c4)

                    def c5():
                        P.op(G, lambda e: e.tensor_tensor(out=xm[b][:], in0=xm[b][:], in1=b1_r[:], op=ALU.add), [("xm", b), "b1_r"], [("xm", b)])
                        P.op(A_, lambda e: e.activation(out=xms[b][:], in_=xm[b][:], func=AF.Identity, bias=0.0, scale=ALPHA), [("xm", b)], [("xms", b)])
                        P.dma(SY, acc_d[rs_, :], xms[b][:], reads=[("xms", b)], writes=[("acc_z", tcn)])
                    ch.append(c5)

                    def c6():
                        st["k2"] = ln_rstd(xm[b], b, 1, [("xm", b)])
                    ch.append(c6)

                    def c7():
                        P.op(V, lambda e: e.tensor_scalar(out=hmf[b][:], in0=xm[b][:], scalar1=mvc[b][1][:, 0:1], scalar2=rstdc[b][1][:], op0=ALU.subtract, op1=ALU.mult),
                             [("xm", b)] + st["k2"], [("hmf", b)])
                        P.op(V, lambda e: e.tensor_tensor(out=hmf[b][:], in0=hmf[b][:], in1=sc2p[j][:], op=ALU.mult), [("hmf", b), ("rows", 2, j)], [("hmf", b)])
                    ch.append(c7)

                    def c8():
                        P.op(G, lambda e: e.tensor_tensor(out=hmf[b][:], in0=hmf[b][:], in1=sh2r[j][:], op=ALU.add), [("hmf", b), ("rows", 1, j)], [("hmf", b)])
                    ch.append(c8)

                    def c9():
                        P.op(A_, lambda e: e.activation(out=hmb[b][:], in_=hmf[b][:], func=AF.Identity, bias=0.0, scale=1.0), [("hmf", b)], [("hmb", b)])
                        P.dma(SY, hm_d[rs_, :], hmb[b][:], reads=[("hmb", b)], writes=[("hm_d", tcn)])
                        for kc in range(8):
                            P.op(T_, lambda e, kc=kc: e.transpose(out=pT2[:, kc * 128:(kc + 1) * 128], in_=hmf[b][:, kc * 128:(kc + 1) * 128], identity=ident_f[:]),
                                 [("hmf", b), "ident_f"], ["pT2"])
                    ch.append(c9)

                    def c10():
                        P.op(A_, lambda e: e.activation(out=hmT[b][:, 0:512], in_=pT2[:, 0:512], func=AF.Identity, bias=0.0, scale=1.0), ["pT2"], [("hmT", b)])
                        P.op(V, lambda e: e.tensor_copy(out=hmT[b][:, 512:1024], in_=pT2[:, 512:1024]), ["pT2"], [("hmT", b)])
                    ch.append(c10)

                    def c11():
                        for kc in range(8):
                            P.op(T_, lambda e, kc=kc: e.matmul(pL[b][:, 0:NE], lhsT=hmT[b][:, kc * 128:(kc + 1) * 128], rhs=wr[:, kc, :], start=(kc == 0), stop=(kc == 7)),
                                 [("hmT", b), "wr"], [("pL", b)])
                    ch.append(c11)

                    def c12():
                        P.op(V, lambda e: e.tensor_reduce(out=mx[b][:], in_=pL[b][:, 0:NE], axis=AX.X, op=ALU.max, negate=True), [("pL", b)], [("mx", b)])
                        P.op(A_, lambda e: e.activation(out=ex[b][:], in_=pL[b][:, 0:NE], func=AF.Exp, bias=mx[b][:], scale=1.0, accum_out=sm[b][:]),
                             [("pL", b), ("mx", b)], [("ex", b), ("sm", b)], multi=True)
                    ch.append(c12)

                    def c13():
                        P.op(V, lambda e: e.reciprocal(out=sm[b][:], in_=sm[b][:]), [("sm", b)], [("sm", b)])
                        P.op(V, lambda e: e.tensor_scalar(out=aff_all[:, tcn, :], in0=ex[b][:], scalar1=sm[b][:], scalar2=None, op0=ALU.mult), [("ex", b), ("sm", b)], [("aff_all", tcn)])
                        P.op(T_, lambda e: e.transpose(out=pL[b][0:NE, 128:256], in_=aff_all[:, tcn, :], identity=ident_f[:]), [("aff_all", tcn), "ident_f"], [("pL", b)])
                    ch.append(c13)

                    def c14():
                        P.op(V, lambda e: e.tensor_copy(out=affT[:, rs_], in_=pL[b][0:NE, 128:256]), [("pL", b)], ["affT"])
                    ch.append(c14)
                    return ch

                interleave([c_chain(t_) for t_ in range(nch_out)], NC3, stagger=5)
              P.barrier()
              dump("affT%d" % l, affT[:], [16, NT], F32, "affT")
              check_stop("C_%d" % l)

            sy.close()
            with contextlib.ExitStack() as sd:
              maskT = sb(sd, "maskT", [16, NT])
              m8 = sb(sd, "m8", [16, 8])
              m8c = sb(sd, "m8c", [16, 8])
              Mf = sb(sd, "Mf", [128, NCH, NE])
              Mb = sb(sd, "Mb", [128, NCH, NE], BF16)
              pos = sb(sd, "pos", [128, NCH, NE])
              tvall = sb(sd, "tvall", [128, NCH, NE, 5], BF16)
              Pe = [sb(sd, "Pe%d" % i, [128, NCH, 256], BF16) for i in range(1)]
              idxf = sb(sd, "idxf", [128, 4])
              idxi = [sb(sd, "idxi%d" % i, [128, 4], I32) for i in range(3)]
              gsl = [sb(sd, "gsl%d" % i, [128, 4]) for i in range(3)]
              nsl = 2 if last else 3
              NTOK = 256 if last else 288
              xs = [[sb(sd, "xs%d_%d" % (i, s_), [128, D], BF16) for s_ in range(nsl)] for i in range(2)]
              xsT = [sb(sd, "xsT%d" % i, [128, 8, 288], BF16) for i in range(2)]
              NWT = 7
              Wt = [sb(sd, "Wt%d" % i, [128, 8, D], BF16) for i in range(NWT)]
              actT = [sb(sd, "actT%d" % i, [128, 8, 288], BF16) for i in range(2)]
              sgtd = [sb(sd, "sgtd%d" % i, [128, 288]) for i in range(2)]
              ye = [[sb(sd, "ye%d_%d" % (i, s_), [128, D]) for s_ in range(nsl)] for i in range(1)]
              pI = pst(sd, "pI", [128, 512])
              pX = pst(sd, "pX", [128, 1024], BF16)
              pGt = [pst(sd, "pGt%d" % i, [128, 512]) for i in range(2)]
              pU = [pst(sd, "pU%d" % i, [128, 512]) for i in range(2)]
              pY = pst(sd, "pY", [128, 1024])
              nchm = nch_out
              for i_ in range(1):
                  for s_ in range(nsl):
                      P.op(V, lambda e, s_=s_, i_=i_: e.memset(ye[i_][s_][:], 0.0), [], [("ye", i_, s_)])
              P.op(V, lambda e: e.tensor_copy(out=maskT[:, 0:S], in_=affT[:, 0:S]), ["affT"], ["maskT"])
              for r in range(CAP // 8):
                  P.op(V, lambda e: e.max(out=m8[:], in_=maskT[:, 0:S]), ["maskT"], ["m8"])
                  if r < CAP // 8 - 1:
                      P.op(V, lambda e: e.match_replace(out=maskT[:, 0:S], in_to_replace=m8[:], in_values=maskT[:, 0:S], imm_value=-1.0), ["maskT", "m8"], ["maskT"])
              P.op(V, lambda e: e.tensor_scalar(out=maskT[:, 0:S], in0=affT[:, 0:S], scalar1=m8[:, 7:8], scalar2=None, op0=ALU.is_ge), ["affT", "m8"], ["maskT"])
              if not last:
                  P.op(V, lambda e: e.tensor_copy(out=maskT[:, S:NT], in_=affT[:, S:NT]), ["affT"], ["maskT"])
                  for r in range(CAPC // 8):
                      P.op(V, lambda e: e.max(out=m8c[:], in_=maskT[:, S:NT]), ["maskT"], ["m8c"])
                      if r < CAPC // 8 - 1:
                          P.op(V, lambda e: e.match_replace(out=maskT[:, S:NT], in_to_replace=m8c[:], in_values=maskT[:, S:NT], imm_value=-1.0), ["maskT", "m8c"], ["maskT"])
                  P.op(V, lambda e: e.tensor_scalar(out=maskT[:, S:NT], in0=affT[:, S:NT], scalar1=m8c[:, 7:8], scalar2=None, op0=ALU.is_ge), ["affT", "m8c"], ["maskT"])
              for tcn in range(nchm):
                  P.op(T_, lambda e, tcn=tcn: e.transpose(out=pI[:, tcn * NE:(tcn + 1) * NE], in_=maskT[:, tcn * 128:(tcn + 1) * 128], identity=ident_f[0:NE, 0:NE]),
                       ["maskT", "ident_f"], ["pI"])
              P.op(V, lambda e: e.tensor_copy(out=Mf[:, 0:nchm, :], in_=pI[:, 0:nchm * NE].rearrange("p (c e) -> p c e", e=NE)), ["pI"], ["Mf"])
              P.op(A_, lambda e: e.activation(out=Mb[:, 0:nchm, :], in_=pI[:, 0:nchm * NE].rearrange("p (c e) -> p c e", e=NE), func=AF.Identity, bias=0.0, scale=1.0), ["pI"], ["Mb"])
              for seg in ([range(16)] if last else [range(16), range(16, 18)]):
                  for c in seg:
                      prev = [c2 for c2 in seg if c2 < c]
                      for c2 in prev:
                          P.op(T_, lambda e, c=c, c2=c2, first=(c2 == prev[0]): e.matmul(pI[:, c * NE:(c + 1) * NE], lhsT=ones_b[:], rhs=Mb[:, c2, :], start=first, stop=False), ["ones_b", "Mb"], ["pI"])
                      P.op(T_, lambda e, c=c, first=(len(prev) == 0): e.matmul(pI[:, c * NE:(c + 1) * NE], lhsT=tri_b[:], rhs=Mb[:, c, :], start=first, stop=True), ["tri_b", "Mb"], ["pI"])
              P.op(V, lambda e: e.tensor_copy(out=pos[:, 0:nchm, :], in_=pI[:, 0:nchm * NE].rearrange("p (c e) -> p c e", e=NE)), ["pI"], ["pos"])
              for e_ in range(NE):
                  P.op(V, lambda e, e_=e_: e.tensor_copy(out=tvall[:, :, e_, 0:2], in_=tvcp_f[:]), ["tvcp_f"], ["tvall"])
              AA = [("aff_all", t) for t in range(nchm)]
              na = nchm
              P.op(V, lambda e: e.tensor_copy(out=tvall[:, 0:na, :, 2], in_=aff_all[:, 0:na, :]), AA, ["tvall"])
              P.op(V, lambda e: e.tensor_tensor(out=aff_all[:, 0:na, :], in0=aff_all[:, 0:na, :], in1=tvall[:, 0:na, :, 2], op=ALU.subtract), AA + ["tvall"], AA)
              P.op(V, lambda e: e.tensor_copy(out=tvall[:, 0:na, :, 3], in_=aff_all[:, 0:na, :]), AA, ["tvall"])
              P.op(V, lambda e: e.tensor_tensor(out=aff_all[:, 0:na, :], in0=aff_all[:, 0:na, :], in1=tvall[:, 0:na, :, 3], op=ALU.subtract), AA + ["tvall"], AA)
              P.op(V, lambda e: e.tensor_copy(out=tvall[:, 0:na, :, 4], in_=aff_all[:, 0:na, :]), AA, ["tvall"])

              wsrc = [w_gate, w_up, w_down]

              def load_expert_w(e_, which):
                  s_ = 3 * e_ + which
                  P.dma(G, Wt[s_ % NWT][:], wsrc[which][l][e_].rearrange("(kc p) n -> p kc n", p=128), writes=[("Wt", s_ % NWT)])

              def route(e_):
                  eb = e_ % 2
                  for c in range(16):
                      P.op(V, lambda e, c=c: e.tensor_scalar(out=Pe[0][:, c, :], in0=iota_f[:], scalar1=pos[:, c, e_:e_ + 1], scalar2=Mf[:, c, e_:e_ + 1],
                                                             op0=ALU.is_equal, op1=ALU.mult), ["iota_f", "pos", "Mf"], [("Pe", 0)])
                  if not last:
                      for c in (16, 17):
                          P.op(V, lambda e, c=c: e.tensor_scalar(out=Pe[0][:, c, 0:128], in0=iota_f[:, 0:128], scalar1=pos[:, c, e_:e_ + 1], scalar2=Mf[:, c, e_:e_ + 1],
                                                                 op0=ALU.is_equal, op1=ALU.mult), ["iota_f", "pos", "Mf"], [("Pe", 0)])
                  for s_ in range(nsl):
                      cs = range(16) if s_ < 2 else (16, 17)
                      col0 = (s_ % 2) * 128 if s_ < 2 else 0
                      for i, c in enumerate(cs):
                          P.op(T_, lambda e, s_=s_, c=c, i=i, n=len(cs), col0=col0: e.matmul(pI[:, 8 * s_:8 * s_ + 5], lhsT=Pe[0][:, c, col0:col0 + 128], rhs=tvall[:, c, e_, :],
                                                                                         start=(i == 0), stop=(i == n - 1)), [("Pe", 0), "tvall"], ["pI"])
                  for s_ in range(nsl):
                      if s_ < 2:
                          P.op(V, lambda e, s_=s_: e.tensor_scalar(out=idxf[:, s_:s_ + 1], in0=pI[:, 8 * s_:8 * s_ + 1], scalar1=128.0, scalar2=pI[:, 8 * s_ + 1:8 * s_ + 2],
                                                                   op0=ALU.mult, op1=ALU.add), ["pI"], ["idxf"])
                      else:
                          P.op(V, lambda e, s_=s_: e.tensor_scalar(out=idxf[:, s_:s_ + 1], in0=pI[:, 8 * s_:8 * s_ + 1], scalar1=128.0, scalar2=pI[:, 8 * s_ + 1:8 * s_ + 2],
                                                                   op0=ALU.mult, op1=ALU.add), ["pI"], ["idxf"])
                          P.op(V, lambda e, s_=s_: e.tensor_tensor(out=idxf[:, s_:s_ + 1], in0=idxf[:, s_:s_ + 1], in1=dumoff[:], op=ALU.add), ["idxf", "dumoff"], ["idxf"])
                      P.op(V, lambda e, s_=s_: e.tensor_reduce(out=gsl[e_ % 3][:, s_:s_ + 1], in_=pI[:, 8 * s_ + 2:8 * s_ + 5], axis=AX.X, op=ALU.add), ["pI"], [("gsl", e_ % 3)])
                  P.op(V, lambda e: e.tensor_copy(out=idxi[e_ % 3][:, 0:nsl], in_=idxf[:, 0:nsl]), ["idxf"], [("idxi", e_ % 3)])
                  for s_ in range(nsl):
                      P.op(G, lambda e, s_=s_: e.indirect_dma_start(out=xs[eb][s_][:], out_offset=None, in_=hm_d,
                                                                    in_offset=bass.IndirectOffsetOnAxis(ap=idxi[e_ % 3][:, s_:s_ + 1], axis=0)),
                           [("hm_d", t_) for t_ in range(nchm)] + ["hm_pad", ("idxi", e_ % 3)], [("xs", eb, s_)], dma=True)

              def compute(e_):
                  eb = e_ % 2
                  wg, wu, wd = Wt[(3 * e_) % NWT], Wt[(3 * e_ + 1) % NWT], Wt[(3 * e_ + 2) % NWT]
                  kg, ku, kd = ("Wt", (3 * e_) % NWT), ("Wt", (3 * e_ + 1) % NWT), ("Wt", (3 * e_ + 2) % NWT)
                  for s_ in range(nsl):
                      w_ = 128 if s_ < 2 else 32
                      for kc in range(8):
                          P.op(T_, lambda e, s_=s_, kc=kc: e.transpose(out=pX[:, kc * 128:(kc + 1) * 128], in_=xs[eb][s_][:, kc * 128:(kc + 1) * 128], identity=ident_b[:]),
                               [("xs", eb, s_), "ident_b"], ["pX"])
                      P.op(A_, lambda e, s_=s_, w_=w_: e.activation(out=xsT[eb][:, :, s_ * 128:s_ * 128 + w_], in_=pX[:].rearrange("p (k t) -> p k t", k=8)[:, :, 0:w_],
                                                                  func=AF.Identity, bias=0.0, scale=1.0), ["pX"], [("xsT", eb)])
                  for fc in range(8):
                      fb = fc % 2
                      for kc in range(8):
                          P.op(T_, lambda e, fc=fc, kc=kc, fb=fb: e.matmul(pGt[fb][:, 0:NTOK], lhsT=wg[:, kc, fc * 128:(fc + 1) * 128], rhs=xsT[eb][:, kc, 0:NTOK],
                                                                         start=(kc == 0), stop=(kc == 7)), [kg, ("xsT", eb)], [("pGt", fb)])
                      for kc in range(8):
                          P.op(T_, lambda e, fc=fc, kc=kc, fb=fb: e.matmul(pU[fb][:, 0:NTOK], lhsT=wu[:, kc, fc * 128:(fc + 1) * 128], rhs=xsT[eb][:, kc, 0:NTOK],
                                                                         start=(kc == 0), stop=(kc == 7)), [ku, ("xsT", eb)], [("pU", fb)])
                      P.op(A_, lambda e, fb=fb: e.activation(out=sgtd[fb][:, 0:NTOK], in_=pGt[fb][:, 0:NTOK], func=AF.Silu), [("pGt", fb)], [("sgtd", fb)])
                      P.op(V, lambda e, fb=fb, fc=fc: e.tensor_tensor(out=actT[eb][:, fc, 0:NTOK], in0=sgtd[fb][:, 0:NTOK], in1=pU[fb][:, 0:NTOK], op=ALU.mult),
                           [("sgtd", fb), ("pU", fb)], [("actT", eb)])

              def compute_down(e_):
                  eb = e_ % 2
                  wd = Wt[(3 * e_ + 2) % NWT]
                  kd = ("Wt", (3 * e_ + 2) % NWT)
                  prevk = [("acc_z", c2) for c2 in range(nchm)] if e_ == 0 else [("acc_e", e_ - 1, s2) for s2 in range(nsl)]
                  for s_ in range(nsl):
                      rws = 128 if s_ < 2 else 32
                      for nh in range(2):
                          for fc in range(8):
                              P.op(T_, lambda e, s_=s_, nh=nh, fc=fc, rws=rws: e.matmul(pY[0:rws, nh * 512:(nh + 1) * 512], lhsT=actT[eb][:, fc, s_ * 128:s_ * 128 + rws],
                                                                                      rhs=wd[:, fc, nh * 512:(nh + 1) * 512], start=(fc == 0), stop=(fc == 7)),
                                   [kd, ("actT", eb)], [("pY", nh)])
                          P.op(A_, lambda e, s_=s_, rws=rws, nh=nh: e.activation(out=ye[0][s_][0:rws, nh * 512:(nh + 1) * 512], in_=pY[0:rws, nh * 512:(nh + 1) * 512], func=AF.Identity,
                                                                               bias=0.0, scale=gsl[e_ % 3][0:rws, s_:s_ + 1]),
                               [("pY", nh), ("gsl", e_ % 3)], [("ye", 0, s_)])
                      jg = 0 if s_ < 2 else 1
                      P.op(G, lambda e, s_=s_, rws=rws, jg=jg: e.tensor_tensor(out=ye[0][s_][0:rws, :], in0=ye[0][s_][0:rws, :], in1=gt2p[jg][0:rws, :], op=ALU.mult),
                           [("ye", 0, s_), ("rows", 3, jg)], [("ye", 0, s_)])
                  for s_ in range(nsl):
                      P.op(G, lambda e, s_=s_: e.indirect_dma_start(out=acc_d, out_offset=bass.IndirectOffsetOnAxis(ap=idxi[e_ % 3][:, s_:s_ + 1], axis=0),
                                                                    in_=ye[0][s_][:], in_offset=None, compute_op=ALU.add),
                           [("ye", 0, s_), ("idxi", e_ % 3), "acc_pad"] + prevk, [("acc_e", e_, s_)], dma=True)

              for wh in range(3):
                  load_expert_w(0, wh)
              route(0)
              for e_ in range(NE):
                  if e_ + 1 < NE:
                      route(e_ + 1)
                      load_expert_w(e_ + 1, 0)
                      load_expert_w(e_ + 1, 1)
                  compute(e_)
                  if e_ + 1 < NE:
                      load_expert_w(e_ + 1, 2)
                  compute_down(e_)
            P.barrier()
            check_stop("D_%d" % l)

            with contextlib.ExitStack() as se:
                g2_r = sb(se, "g2_r", [128, D])
                b2_r = sb(se, "b2_r", [128, D])
                NB = 6
                ac = [sb(se, "ac%d" % i, [128, D]) for i in range(NB)]
                xo = [sb(se, "xo%d" % i, [128, D]) for i in range(NB)]
                st6e = [sb(se, "st6e%d" % i, [128, 2, 6]) for i in range(NB)]
                mve = [sb(se, "mve%d" % i, [128, 2]) for i in range(NB)]
                vee = [sb(se, "vee%d" % i, [128, 1]) for i in range(NB)]
                lne = [sb(se, "lne%d" % i, [128, 1]) for i in range(NB)]
                rstde = [sb(se, "rstde%d" % i, [128, 1]) for i in range(NB)]
                P.dma(A_, g2_r[:], prow_d[l][:, PR_G2:PR_G2 + D], writes=["g2_r"])
                P.dma(A_, b2_r[:], prow_d[l][:, PR_B2:PR_B2 + D], writes=["b2_r"])

                def e_chain(tcn):
                    j = 0 if tcn < 16 else 1
                    b = tcn % NB
                    rs_ = slice(tcn * 128, (tcn + 1) * 128)
                    ch = []

                    def e0():
                        P.dma(A_, ac[b][:], acc_d[rs_, :], reads=[("acc_e", NE - 1, s2) for s2 in range(2 if last else 3)] + [("acc_z", tcn)], writes=[("ac", b)])
                    ch.append(e0)

                    def e1():
                        P.op(V, lambda e: e.bn_stats(out=st6e[b][:, 0, :], in_=ac[b][:, 0:512]), [("ac", b)], [("st6e", b)])
                        P.op(V, lambda e: e.bn_stats(out=st6e[b][:, 1, :], in_=ac[b][:, 512:1024]), [("ac", b)], [("st6e", b)])
                        P.op(V, lambda e: e.bn_aggr(out=mve[b][:], in_=st6e[b][:].rearrange("p a b -> p (a b)")), [("st6e", b)], [("mve", b)])
                        P.op(V, lambda e: e.tensor_scalar(out=vee[b][:], in0=mve[b][:, 1:2], scalar1=LN_EPS, scalar2=None, op0=ALU.add), [("mve", b)], [("vee", b)])
                    ch.append(e1)

                    def e2():
                        P.op(A_, lambda e: e.activation(out=lne[b][:], in_=vee[b][:], func=AF.Ln), [("vee", b)], [("lne", b)])
                        P.op(A_, lambda e: e.activation(out=rstde[b][:], in_=lne[b][:], func=AF.Exp, scale=-0.5), [("lne", b)], [("rstde", b)])
                    ch.append(e2)

                    def e3():
                        P.op(V, lambda e: e.tensor_scalar(out=xo[b][:], in0=ac[b][:], scalar1=mve[b][:, 0:1], scalar2=rstde[b][:], op0=ALU.subtract, op1=ALU.mult),
                             [("ac", b), ("mve", b), ("rstde", b)], [("xo", b)])
                        P.op(V, lambda e: e.tensor_tensor(out=xo[b][:], in0=xo[b][:], in1=g2_r[:], op=ALU.mult), [("xo", b), "g2_r"], [("xo", b)])
                    ch.append(e3)

                    def e4():
                        P.op(G, lambda e: e.tensor_tensor(out=xo[b][:], in0=xo[b][:], in1=b2_r[:], op=ALU.add), [("xo", b), "b2_r"], [("xo", b)])
                        if last:
                            P.dma(SY, out_d[rs_, :], xo[b][:], reads=[("xo", b)], writes=[("outd", tcn)])
                        else:
                            P.dma(SY, xcur_d[rs_, :], xo[b][:], reads=[("xo", b)], writes=[("xcur", tcn // 4)])
                    ch.append(e4)
                    return ch

                interleave([e_chain(t_) for t_ in range(nch_out)], 6, stagger=1)
            P.barrier()
            if not last:
                dump("xcur%d" % l, xcur_d, [NT, D], F32, [("xcur", t_) for t_ in range(5)])
            check_stop("E_%d" % l)

        for l_ in range(DEPTH):
            layer(l_)

    try:
        body()
    except Stop:
        pass
    P.emit()
    return nc, dump_out


def _host_prep(inputs):
    f = np.float32
    g = {k: np.ascontiguousarray(np.asarray(v, dtype=f)) for k, v in inputs.items()}
    shared = {}
    for k in ["w_mod", "w_in", "w_out", "w_router", "w_gate", "w_up", "w_down"]:
        shared[k] = g[k]
    pcol = np.zeros((128, DEPTH, NPC), f)
    prow = np.zeros((DEPTH, 128, NPR), f)
    for l in range(DEPTH):
        pcol[:, l, PC_BMOD:PC_BMOD + 16] = g["b_mod"][l, :2048].reshape(16, 128).T
        pcol[:, l, PC_BIN:PC_BIN + 18] = g["b_in"][l, :2304].reshape(18, 128).T
        for j in range(2):
            pcol[:, l, PC_WSH + j * 3:PC_WSH + j * 3 + 3] = g["w_short"][l][:, j * 128:(j + 1) * 128].T
            pcol[:, l, PC_WCF + j * 31:PC_WCF + j * 31 + 31] = g["w_conf_dw"][l][:, j * 128:(j + 1) * 128].T
            pcol[:, l, PC_BCF + j] = g["b_conf_dw"][l, j * 128:(j + 1) * 128]
            pcol[:, l, PC_GLN + j] = g["g_conf_ln"][l, j * 128:(j + 1) * 128]
            pcol[:, l, PC_BLN + j] = g["b_conf_ln"][l, j * 128:(j + 1) * 128]
        row = np.concatenate([g["b_mod"][l, 2048:], g["b_in"][l, 2304:], g["b_out"][l], g["g_post1"][l], g["b_post1"][l], g["g_post2"][l], g["b_post2"][l]])
        prow[l] = np.broadcast_to(row[None, :], (128, NPR))
    kc = np.arange(64)[:, None]
    qc = np.arange(64)[None, :]
    dcol = np.clip(kc - qc + 15, 0, 30)
    rpbt = np.zeros((DEPTH, 128, 8, 896), f)
    for l in range(DEPTH):
        for j in range(14):
            d = 6 - j
            rpbt[l, 0:64, :, j * 64:(j + 1) * 64] = np.transpose(g["na_rpb"][l][:, d + 7][:, dcol], (1, 0, 2))
            rpbt[l, 64:128, :, j * 64:(j + 1) * 64] = np.transpose(g["na_rpb"][l][:, d + 8][:, dcol], (1, 0, 2))
    cstart = np.clip(np.arange(64) - 8, 0, 48)
    col_in = (kc >= cstart[None, :]) & (kc < cstart[None, :] + 16)
    cm1 = np.where(col_in, 0.0, -1e30).astype(f)
    cmask = np.tile(np.concatenate([cm1, cm1], 0), (1, 14))
    shared.update(pcol=pcol, prow=prow, rpbt=rpbt, cmask=np.ascontiguousarray(cmask),
                  ident=np.eye(128, dtype=f), iota=np.ascontiguousarray(np.broadcast_to(np.arange(256, dtype=f)[None], (128, 256))),
                  tri=np.triu(np.ones((128, 128), f), 1))
    tvcp = np.zeros((128, NCH, 2), f)
    tvcp[:, :, 0] = np.arange(NCH)[None, :]
    tvcp[:, :, 1] = np.arange(128)[:, None]
    dumoff = np.zeros((128, 1), f)
    dumoff[32:, 0] = NT + np.arange(96)
    shared.update(tvcp=tvcp, dumoff=dumoff)
    in_maps = []
    for b in range(8):
        m = dict(shared)
        m["xin"] = np.ascontiguousarray(np.concatenate([g["x"][b], g["ctx"][b]], 0))
        cc = np.stack([g["c"][b], g["c_ctx"]], 0)
        m["ccT"] = np.ascontiguousarray(cc.reshape(2, 8, 128).transpose(2, 1, 0))
        in_maps.append(m)
    return in_maps


_NC_CACHE = {}


def kernel(**inputs):
    in_maps = _host_prep(inputs)
    if "nc" not in _NC_CACHE:
        _NC_CACHE["nc"] = build()[0]
    nc = _NC_CACHE["nc"]
    res = run_bass_kernel_spmd(nc, in_maps, core_ids=list(range(8)))
    return np.stack([np.asarray(r["out"], dtype=np.float32) for r in res.results], 0)
```
